# Optimizing a Trainium2 kernel written in Bass

```python
import math
import jax, jax.numpy as jnp
from jax import lax
import numpy as np


D_MODEL = 1024
BATCH = 4
SEQ = 8192
DEPTH = 1

CHUNK = 64
EPS = 1e-6
A_HEADS = 8
A_DK = 128
A_DV = 256
CONV_W = 4
B_HEADS = 4
B_DK = 256
B_DV = 512
ROPE_BASE = 10000.0
D_FF = 4 * D_MODEL

A_QK = A_HEADS * A_DK
A_V = A_HEADS * A_DV
B_QK = B_HEADS * B_DK
B_V = B_HEADS * B_DV
IN_SIZES = (A_QK, A_QK, A_V, A_V, A_HEADS, A_HEADS, B_QK, B_QK, B_V, B_V, D_MODEL, D_MODEL)
D_IN = sum(IN_SIZES)

kernel_name = "hybrid_gdn_retention_sandwich_block"


def rmsnorm(x, w):
    xf = x.astype(jnp.float32)
    y = xf * lax.rsqrt(jnp.mean(xf * xf, axis=-1, keepdims=True) + EPS)
    return (y * w.astype(jnp.float32)).astype(x.dtype)


def l2norm(x):
    xf = x.astype(jnp.float32)
    return (xf * lax.rsqrt(jnp.sum(xf * xf, axis=-1, keepdims=True) + EPS)).astype(x.dtype)


def _split_cols(t, sizes):
    out, start = [], 0
    for s in sizes:
        out.append(t[..., start:start + s])
        start += s
    return out


def causal_depthwise_conv(x, w):
    k_taps, s = w.shape[0], x.shape[1]
    xp = jnp.pad(x, ((0, 0), (k_taps - 1, 0), (0, 0)))
    y = xp[:, 0:s] * w[0]
    for j in range(1, k_taps):
        y = y + xp[:, j:j + s] * w[j]
    return y


def rotary(t, positions):
    d = t.shape[-1]
    inv_freq = ROPE_BASE ** (-jnp.arange(0, d, 2, dtype=jnp.float32) / d)
    ang = positions.astype(jnp.float32)[:, None] * inv_freq[None, :]
    cos = jnp.cos(ang)[None, :, None, :]
    sin = jnp.sin(ang)[None, :, None, :]
    t1, t2 = t[..., :d // 2], t[..., d // 2:]
    return jnp.concatenate([t1 * cos - t2 * sin, t1 * sin + t2 * cos], axis=-1).astype(t.dtype)


def to_chunks(t):
    b, s, h = t.shape[:3]
    t = t.reshape(b, s // CHUNK, CHUNK, h, *t.shape[3:])
    return jnp.moveaxis(t, 3, 1)


def from_chunks(t):
    t = jnp.moveaxis(t, 1, 3)
    return t.reshape(t.shape[0], t.shape[1] * t.shape[2], *t.shape[3:])


def chunk_gated_delta_rule(q, k, v, beta, g):
    dtype = v.dtype
    q, k, v = (to_chunks(t.astype(jnp.float32)) for t in (q, k, v))
    beta, g = (to_chunks(t.astype(jnp.float32)) for t in (beta, g))
    g_cum = jnp.cumsum(g, axis=-1)
    idx = jnp.arange(CHUNK)
    causal = idx[:, None] >= idx[None, :]
    strict = idx[:, None] > idx[None, :]
    decay = jnp.exp(jnp.where(causal, g_cum[..., :, None] - g_cum[..., None, :], -jnp.inf))
    k_beta = k * beta[..., None]
    lower = jnp.where(strict, jnp.einsum('bhncd,bhnmd->bhncm', k_beta, k) * decay, 0.0)
    eye = jnp.broadcast_to(jnp.eye(CHUNK, dtype=jnp.float32), lower.shape)
    t_inv = lax.linalg.triangular_solve(lower, eye, left_side=True, lower=True, unit_diagonal=True)
    u = jnp.einsum('bhncm,bhnme->bhnce', t_inv, v * beta[..., None])
    w = jnp.einsum('bhncm,bhnmd->bhncd', t_inv, k_beta * jnp.exp(g_cum)[..., None])
    attn = jnp.where(causal, jnp.einsum('bhncd,bhnmd->bhncm', q, k) * decay, 0.0)
    q_exp = q * jnp.exp(g_cum)[..., None]
    g_last = g_cum[..., -1:]
    k_dec = k * jnp.exp(g_last - g_cum)[..., None]
    last_dec = jnp.exp(g_last[..., 0])

    def step(state, inp):
        qe, ww, uu, aqk, kd, ld = inp
        v_new = uu - jnp.einsum('bhck,bhkv->bhcv', ww, state)
        o = jnp.einsum('bhck,bhkv->bhcv', qe, state) + jnp.einsum('bhcm,bhmv->bhcv', aqk, v_new)
        state = state * ld[..., None, None] + jnp.einsum('bhck,bhcv->bhkv', kd, v_new)
        return state, o

    b, h = q.shape[0], q.shape[1]
    state0 = jnp.zeros((b, h, q.shape[-1], v.shape[-1]), jnp.float32)
    xs = tuple(jnp.moveaxis(t, 2, 0) for t in (q_exp, w, u, attn, k_dec, last_dec))
    _, o = lax.scan(step, state0, xs)
    return from_chunks(jnp.moveaxis(o, 0, 2)).astype(dtype)


def chunk_retention(q, k, v):
    dtype = v.dtype
    q, k, v = (to_chunks(t.astype(jnp.float32)) for t in (q, k, v))
    h = q.shape[1]
    log_gamma = jnp.log1p(-jnp.exp2(-5.0 - jnp.arange(h, dtype=jnp.float32)))
    pos = jnp.arange(CHUNK, dtype=jnp.float32)
    dist = jnp.abs(pos[:, None] - pos[None, :])
    intra = jnp.exp(log_gamma[:, None, None] * dist)
    scores = jnp.einsum('bhncd,bhnmd->bhncm', q, k) * intra[None, :, None]
    o_intra = jnp.einsum('bhncm,bhnme->bhnce', scores, v)
    q_dec = jnp.exp(log_gamma[:, None] * (pos + 1.0))[None, :, :, None]
    k_dec = jnp.exp(log_gamma[:, None] * (CHUNK - 1.0 - pos))[None, :, :, None]
    chunk_dec = jnp.exp(log_gamma * CHUNK)[None, :, None, None]

    def step(state, inp):
        qc, kc, vc = inp
        o = jnp.einsum('bhcd,bhde->bhce', qc * q_dec, state)
        state = state * chunk_dec + jnp.einsum('bhcd,bhce->bhde', kc * k_dec, vc)
        return state, o

    state0 = jnp.zeros((q.shape[0], h, q.shape[-1], v.shape[-1]), jnp.float32)
    xs = tuple(jnp.moveaxis(t, 2, 0) for t in (q, k, v))
    _, o_inter = lax.scan(step, state0, xs)
    o = o_intra + jnp.moveaxis(o_inter, 0, 2)
    return from_chunks(o).astype(dtype)


def hybrid_layer(x, n_mix_pre, n_mix_post, n_mlp_pre, n_mlp_post, w_in, conv_a, a_log, dt_bias,
                 norm_a, norm_b, w_br_a, w_br_b, w_out, w_up, w_down):
    b, s, _ = x.shape
    xn = rmsnorm(x, n_mix_pre)
    proj = xn @ w_in
    qa, ka, va, za, beta_logit, a_dt, qb, kb, vb, gb, gate_a, gate_b = _split_cols(proj, IN_SIZES)

    qkv = jax.nn.silu(causal_depthwise_conv(jnp.concatenate([qa, ka, va], axis=-1), conv_a))
    qa, ka, va = _split_cols(qkv, (A_QK, A_QK, A_V))
    qa = l2norm(qa.reshape(b, s, A_HEADS, A_DK)) * (A_DK ** -0.5)
    ka = l2norm(ka.reshape(b, s, A_HEADS, A_DK))
    va = va.reshape(b, s, A_HEADS, A_DV)
    beta = jax.nn.sigmoid(beta_logit)
    g = -jnp.exp(a_log) * jax.nn.softplus(a_dt + dt_bias)
    oa = chunk_gated_delta_rule(qa, ka, va, beta, g)
    oa = rmsnorm(oa, norm_a) * jax.nn.silu(za.reshape(b, s, A_HEADS, A_DV))
    ya = oa.reshape(b, s, A_V) @ w_br_a

    positions = jnp.arange(s)
    qb = rotary(qb.reshape(b, s, B_HEADS, B_DK), positions)
    kb = rotary(kb.reshape(b, s, B_HEADS, B_DK), positions) * (B_DK ** -0.5)
    vb = vb.reshape(b, s, B_HEADS, B_DV)
    ob = chunk_retention(qb, kb, vb)
    ob = rmsnorm(ob, norm_b.reshape(B_HEADS, B_DV)) * jax.nn.silu(gb.reshape(b, s, B_HEADS, B_DV))
    yb = ob.reshape(b, s, B_V) @ w_br_b

    mix = (jax.nn.sigmoid(gate_a) * ya + jax.nn.sigmoid(gate_b) * yb) @ w_out
    x = x + rmsnorm(mix, n_mix_post)

    hn = rmsnorm(x, n_mlp_pre)
    ff = jnp.square(jax.nn.relu(hn @ w_up)) @ w_down
    return x + rmsnorm(ff, n_mlp_post)


def setup_inputs(seed: int = 0) -> dict:
    key = jax.random.key(seed)
    ks = jax.random.split(key, 16)
    L = DEPTH

    def nrm(k, shape, fan_in):
        return jax.random.normal(k, shape, jnp.float32) * fan_in ** -0.5

    def gain(k, shape):
        return 1.0 + 0.05 * jax.random.normal(k, shape, jnp.float32)

    x = jax.random.normal(ks[0], (BATCH, SEQ, D_MODEL), jnp.float32)
    dt = jnp.exp(jax.random.uniform(ks[8], (L, A_HEADS), jnp.float32,
                                    minval=math.log(1e-3), maxval=math.log(1e-1)))
    dt_bias = dt + jnp.log(-jnp.expm1(-dt))
    a_log = jnp.log(jax.random.uniform(ks[9], (L, A_HEADS), jnp.float32, minval=1.0, maxval=16.0))
    return {
        "x": x,
        "norm_mix_pre": gain(ks[1], (L, D_MODEL)),
        "norm_mix_post": gain(ks[2], (L, D_MODEL)),
        "norm_mlp_pre": gain(ks[3], (L, D_MODEL)),
        "norm_mlp_post": gain(ks[4], (L, D_MODEL)),
        "w_in": nrm(ks[5], (L, D_MODEL, D_IN), D_MODEL),
        "conv_a": nrm(ks[6], (L, CONV_W, 2 * A_QK + A_V), CONV_W),
        "a_log": a_log,
        "dt_bias": dt_bias,
        "norm_a": gain(ks[7], (L, A_DV)),
        "norm_b": gain(ks[10], (L, B_V)),
        "w_br_a": nrm(ks[11], (L, A_V, D_MODEL), A_V),
        "w_br_b": nrm(ks[12], (L, B_V, D_MODEL), B_V),
        "w_out": nrm(ks[13], (L, D_MODEL, D_MODEL), D_MODEL),
        "w_up": nrm(ks[14], (L, D_MODEL, D_FF), D_MODEL),
        "w_down": nrm(ks[15], (L, D_FF, D_MODEL), D_FF),
    }


def reference(x, norm_mix_pre, norm_mix_post, norm_mlp_pre, norm_mlp_post, w_in, conv_a, a_log,
              dt_bias, norm_a, norm_b, w_br_a, w_br_b, w_out, w_up, w_down):
    for l in range(DEPTH):
        x = hybrid_layer(x, norm_mix_pre[l], norm_mix_post[l], norm_mlp_pre[l], norm_mlp_post[l],
                         w_in[l], conv_a[l], a_log[l], dt_bias[l], norm_a[l], norm_b[l],
                         w_br_a[l], w_br_b[l], w_out[l], w_up[l], w_down[l])
    return x
```

```python
import math
import numpy as np
import concourse.bass as bass
import concourse.mybir as mybir
from concourse.bass_utils import run_bass_kernel_spmd

F32 = mybir.dt.float32
BF16 = mybir.dt.bfloat16
U8 = mybir.dt.uint8
AF = mybir.ActivationFunctionType
ALU = mybir.AluOpType

D = 1024
EPS = 1e-6
C = 64
HA, DKA, DVA = 8, 128, 256
HB, DKB, DVB = 4, 256, 512
DFF = 4096
QA, KA, VA, ZA, BETA, DTC, QB, KB, VB, GB, GTA, GTB = (
    0, 1024, 2048, 4096, 6144, 6152, 6160, 7184, 8208, 10256, 12304, 13328)
DIN = 14352
NEG = -30000.0
ATOM = 256
ARENA = 212736

SAME_SYNC = True
NOSYNC_ENGINES = ("dve",)
DBG = {}


class View:
    __slots__ = ("ap", "atoms")

    def __init__(self, ap, atoms):
        self.ap = ap
        self.atoms = atoms

    def __getitem__(self, key):
        return View(self.ap[key], self.atoms)

    def re(self, s, **kw):
        return View(self.ap.rearrange(s, **kw), self.atoms)

    def un(self, axis):
        return View(self.ap.unsqueeze(axis), self.atoms)

    def bc(self, shape):
        return View(self.ap.to_broadcast(list(shape)), self.atoms)


class Eng:
    def __init__(self, name, h, sem):
        self.name, self.h, self.sem = name, h, sem
        self.cnt = 0
        self.seen = {}


class DSem:
    def __init__(self, h, key):
        self.h, self.key, self.val = h, key, 0


class Trk:
    def __init__(self, nc):
        self.nc = nc
        self.sems = {}
        self.eng = {}
        for name, h in (("pe", nc.tensor), ("dve", nc.vector), ("act", nc.scalar),
                        ("pool", nc.gpsimd)):
            s = nc.alloc_semaphore(name="sem_" + name)
            self.sems[name] = s
            self.eng[name] = Eng(name, h, s)
        self.sp = Eng("sp", nc.sync, None)
        self.eng["sp"] = self.sp
        self.dsems = {}
        self.st = {}
        self.pend_r = []
        self.pend_w = []
        self.n_wait = 0
        self.n_inst = 0

    def dsem(self, name):
        h = self.nc.alloc_semaphore(name="dsem_" + name)
        self.sems["d_" + name] = h
        d = DSem(h, "d_" + name)
        self.dsems["d_" + name] = d
        return d

    def _needs(self, reads, writes):
        need = {}
        for v in reads:
            for a in v.atoms:
                s = self.st.get(a)
                if s:
                    for k, x in s[0].items():
                        if need.get(k, 0) < x:
                            need[k] = x
        for v in writes:
            for a in v.atoms:
                s = self.st.get(a)
                if s:
                    for k, x in s[0].items():
                        if need.get(k, 0) < x:
                            need[k] = x
                    for k, x in s[1].items():
                        if need.get(k, 0) < x:
                            need[k] = x
        return need

    def _sync(self, eng, reads, writes):
        need = self._needs(reads, writes)
        for v in reads:
            for a in v.atoms:
                if a[0] == "ps":
                    s = self.st.get(a)
                    if s:
                        for k, x in s[1].items():
                            if k != eng.name and need.get(k, 0) < x:
                                need[k] = x
        for k, x in need.items():
            if k == eng.name and (eng.name == "pe" or (not SAME_SYNC and eng.name in NOSYNC_ENGINES)):
                continue
            if k in self.dsems:
                x = self.dsems[k].val
            if eng.seen.get(k, 0) >= x:
                continue
            eng.h.wait_ge(self.sems[k], x)
            eng.seen[k] = x
            self.n_wait += 1

    def _record(self, key, val, reads, writes):
        for v in reads:
            for a in v.atoms:
                s = self.st.setdefault(a, [{}, {}])
                if s[1].get(key, 0) < val:
                    s[1][key] = val
        for v in writes:
            for a in v.atoms:
                self.st[a] = [{key: val}, {}]

    def op(self, e, fn, reads, writes):
        eng = self.eng[e]
        self._sync(eng, reads, writes)
        inst = fn(eng.h)
        eng.cnt += 1
        inst.then_inc(eng.sem, 1)
        self._record(eng.name, eng.cnt, reads, writes)
        self.n_inst += 1

    def mm(self, out, lhsT, rhs, start=True, stop=True, last=True, tr=False):
        eng = self.eng["pe"]
        reads, writes = [lhsT, rhs], [out]
        self._sync(eng, reads, writes)
        if tr:
            inst = eng.h.transpose(out.ap, lhsT.ap, rhs.ap)
        else:
            inst = eng.h.matmul(out.ap, lhsT.ap, rhs.ap, start=start, stop=stop)
        self.pend_r += reads
        self.pend_w += writes
        self.n_inst += 1
        if last:
            eng.cnt += 1
            inst.then_inc(eng.sem, 1)
            self._record("pe", eng.cnt, self.pend_r, self.pend_w)
            self.pend_r, self.pend_w = [], []

    def dma(self, pairs, reads, writes, sem, eng="sp", **kw):
        e = self.eng[eng]
        self._sync(e, reads, writes)
        for o, i in pairs:
            inst = e.h.dma_start(out=o, in_=i, **kw)
            sem.val += 16
            inst.then_inc(sem.h, 16)
            self.n_inst += 1
        self._record(sem.key, sem.val, reads, writes)

    def wait_all(self, eng, sems):
        e = self.eng[eng]
        for s in sems:
            if s.val > 0 and e.seen.get(s.key, 0) < s.val:
                e.h.wait_ge(s.h, s.val)
                e.seen[s.key] = s.val


class Arena:
    def __init__(self, nc, nbytes):
        self.cm = nc.sbuf_tensor("arena", [128, nbytes], U8)
        self.t = self.cm.__enter__()
        self.n = nbytes
        self.top = 0

    def alloc(self, nbytes):
        nbytes = (nbytes + ATOM - 1) // ATOM * ATOM
        off = self.top
        self.top += nbytes
        assert self.top <= self.n, f"SBUF arena overflow {self.top} > {self.n}"
        return off

    def view(self, off, dtype, free, parts=128, pat=None, **kw):
        esz = 2 if dtype == BF16 else 4
        nb = free * esz
        self.last = (off, nb)
        ap = self.t[0:parts, off:off + nb].bitcast(dtype)
        if pat:
            ap = ap.rearrange(pat, **kw)
        atoms = tuple(("sb", i) for i in range(off // ATOM, (off + nb - 1) // ATOM + 1))
        return View(ap, atoms)

    def new(self, dtype, free, parts=128, pat=None, **kw):
        esz = 2 if dtype == BF16 else 4
        off = self.alloc(free * esz)
        return self.view(off, dtype, free, parts, pat, **kw)


class Sub:
    def __init__(self, arena, off, size):
        self.a, self.off, self.size, self.top = arena, off, size, 0

    def new(self, dtype, free, parts=128, pat=None, **kw):
        esz = 2 if dtype == BF16 else 4
        nb = (free * esz + ATOM - 1) // ATOM * ATOM
        o = self.off + self.top
        self.top += nb
        assert self.top <= self.size, f"sub-region overflow {self.top} > {self.size}"
        return self.a.view(o, dtype, free, parts, pat, **kw)


def host_consts():
    c64 = {}
    i = np.arange(64)
    c64["U"] = (i[:, None] <= i[None, :]).astype(np.float32)
    c64["UGT"] = (i[:, None] > i[None, :]).astype(np.float32)
    c64["NEG1"] = -np.ones((64, 64), np.float32)
    c64["NML"] = np.where(i[:, None] > i[None, :], 0.0, NEG).astype(np.float32)
    c64["NMUS"] = np.where(i[None, :] > i[:, None], 0.0, NEG).astype(np.float32)
    c64["NMUI"] = np.where(i[None, :] >= i[:, None], 0.0, NEG).astype(np.float32)
    h = np.arange(HB, dtype=np.float64)
    lg = np.log1p(-np.exp2(-5.0 - h))
    pos = np.arange(64, dtype=np.float64)
    dist = np.abs(pos[:, None] - pos[None, :])
    qdec = np.exp(lg[:, None] * (pos + 1.0))
    kdec = np.exp(lg[:, None] * (63.0 - pos))
    intra = np.exp(lg[:, None, None] * dist)
    intp = intra / qdec[:, None, :]
    c64["INTP"] = np.transpose(intp, (1, 0, 2)).reshape(64, HB * 64).astype(np.float32)
    c64["KDEC"] = kdec.T.astype(np.float32).copy()
    names64 = ["U", "UGT", "NEG1", "NML", "NMUS", "NMUI", "INTP", "KDEC"]
    off64, cols = {}, 0
    for n in names64:
        off64[n] = cols
        cols += c64[n].shape[1]
    a64 = np.concatenate([c64[n] for n in names64], axis=1)
    c128 = {}
    c128["IDF"] = np.eye(128, dtype=np.float32)
    c128["ONES"] = np.ones((128, 128), np.float32)
    c128["QDEC"] = np.broadcast_to(qdec.reshape(1, HB * 64), (128, HB * 64)).astype(np.float32)
    names128 = ["IDF", "ONES", "QDEC"]
    off128, cols = {}, 0
    for n in names128:
        off128[n] = cols
        cols += c128[n].shape[1]
    a128 = np.concatenate([c128[n] for n in names128], axis=1)
    cdec = [float(np.exp(lg[k] * 64.0)) for k in range(HB)]
    return a64, off64, a128, off128, cdec


def rope_tables(pos0, n):
    inv = 10000.0 ** (-(np.arange(0, DKB, 2, dtype=np.float32) / np.float32(DKB)))
    inv = inv.astype(np.float32)
    p = (pos0 + np.arange(n)).astype(np.float32)
    ang = (p[None, :] * inv[:, None]).astype(np.float32)
    return np.cos(ang.astype(np.float64)).astype(np.float32), np.sin(ang.astype(np.float64)).astype(np.float32)


def build(NT, NT_WARM, TN=256, upto=None):
    NS, NCH = TN // 128, TN // C
    NTOK = NT * TN
    NOUT = (NT - NT_WARM) * TN
    a64, off64, a128, off128, cdec = host_consts()
    nc = bass.Bass("TRN2", target_bir_lowering=False)

    def din(name, shape, dt=F32):
        return nc.dram_tensor(name, list(shape), dt, kind="ExternalInput").ap()

    x_d = din("x", [NTOK, D])
    w_in_d = din("w_in", [D, DIN])
    w_bra_d = din("w_br_a", [2048, D])
    w_brb_d = din("w_br_b", [2048, D])
    w_out_d = din("w_out", [D, D])
    w_up_d = din("w_up", [D, DFF])
    w_dn_d = din("w_down", [DFF, D])
    conv_d = din("conv_a", [4, 4096])
    alog_d = din("a_log", [1, 8])
    dtb_d = din("dt_bias", [1, 8])
    na_d = din("norm_a", [1, 256])
    nb_d = din("norm_b", [1, 2048])
    npre_d = din("norm_mix_pre", [1, D])
    npost_d = din("norm_mix_post", [1, D])
    mpre_d = din("norm_mlp_pre", [1, D])
    mpost_d = din("norm_mlp_post", [1, D])
    c64_d = din("c64", a64.shape)
    c128_d = din("c128", a128.shape)
    cos_d = din("rcos", [128, NTOK])
    sin_d = din("rsin", [128, NTOK])
    out_d = nc.dram_tensor("out", [NOUT, D], F32, kind="ExternalOutput").ap()

    def dscr(name, shape):
        return nc.dram_tensor(name, list(shape), BF16, kind="Internal").ap()

    win_b = dscr("win_b", [D, DIN])
    wbra_b = dscr("wbra_b", [2048, D])
    wbrb_b = dscr("wbrb_b", [2048, D])
    wout_b = dscr("wout_b", [D, D])
    wup_b = dscr("wup_b", [D, DFF])
    wdn_b = dscr("wdn_b", [DFF, D])

    T = Trk(nc)
    A = Arena(nc, ARENA)
    ps_cm = nc.psum_tensor("psum", [128, 4096], F32)
    ps_t = ps_cm.__enter__()
    ps_ptr = [0]

    ps_pool = [None]

    def PS(nb=1):
        if ps_pool[0] is None:
            lo, hi, ptr = 0, 8, ps_ptr
        else:
            lo, hi, ptr = ps_pool[0]
        b = ptr[0]
        if b < lo or b + nb > hi:
            b = lo
        ptr[0] = b + nb
        return b

    def psv(bank, dtype, free, parts=128, nb=1, pat=None, **kw):
        ap = ps_t[0:parts, bank * 512:(bank + nb) * 512]
        if dtype == BF16:
            ap = ap.bitcast(BF16)
        ap = ap[:, 0:free]
        if pat:
            ap = ap.rearrange(pat, **kw)
        return View(ap, tuple(("ps", bank + j) for j in range(nb)))

    def dv(ap, name):
        return View(ap, (("dram", name),))

    c64v = A.new(F32, a64.shape[1], parts=64)
    c128v = A.new(F32, a128.shape[1])

    def k64(n, w=64):
        return c64v[:, off64[n]:off64[n] + w]

    def k128(n, w=128):
        return c128v[:, off128[n]:off128[n] + w]

    identf = k128("IDF")
    onesf = k128("ONES")
    identb = A.new(BF16, 128)
    epsc = A.new(F32, 1)

    def rsqrt(dst, src, mult):
        P = dst.ap.shape[0]
        T.op("act", lambda e: e.activation(dst.ap, src.ap, AF.Ln, bias=epsc.ap[0:P, :], scale=mult), [src, epsc], [dst])
        T.op("act", lambda e: e.activation(dst.ap, dst.ap, AF.Exp, scale=-0.5), [dst], [dst])

    cw = A.new(F32, 32 * 4, pat="p (j b) -> p j b", j=4)
    wpre = A.new(F32, 8)
    wmlp = A.new(F32, 8)
    nav = A.new(F32, 2)
    nbv = A.new(F32, 16)
    npost_row = A.new(F32, D)
    mpost_row = A.new(F32, D)
    dtb_row = A.new(F32, 8, parts=64)
    alog_row = A.new(F32, 8, parts=64)
    negA = A.new(F32, 8, parts=64)
    wsm_f = A.new(F32, 8 * 16, pat="p (k c) -> p k c", c=16)
    wsm = A.new(BF16, 8 * 16, pat="p (k c) -> p k c", c=16)
    halo = A.new(F32, 32 * 3, pat="p (b j) -> p b j", j=3)
    Sa = A.new(F32, HA * DVA, pat="p (h v) -> p h v", v=DVA)
    Sab = A.new(BF16, HA * DVA, pat="p (h v) -> p h v", v=DVA)
    Sb = [A.new(F32, HB * DVB, pat="p (h v) -> p h v", v=DVB) for _ in range(2)]
    Sbb = [A.new(BF16, HB * DVB, pat="p (h v) -> p h v", v=DVB) for _ in range(2)]
    x_tok = A.new(F32, NS * D, pat="p (s d) -> p s d", d=D)
    xnT = A.new(BF16, 8 * TN, pat="p (k t) -> p k t", t=TN)
    NSLAB = 2
    wslab = [A.new(BF16, 4096) for _ in range(NSLAB)]
    oaT = A.new(BF16, 16 * TN, pat="p (k t) -> p k t", t=TN)
    obT = A.new(BF16, 16 * TN, pat="p (k t) -> p k t", t=TN)
    mixT = A.new(BF16, 8 * TN, pat="p (k t) -> p k t", t=TN)
    cosv = A.new(F32, TN)
    sinv = A.new(F32, TN)
    ssq = A.new(F32, 2 * NS)
    rstd = A.new(F32, NS)
    gsm = {n: A.new(F32, NCH * 8, parts=64, pat="p (c h) -> p c h", h=8)
           for n in ("e1", "l1", "lnb", "beta", "t", "g")}
    glog = A.new(F32, NCH * 16, parts=64, pat="p (c h) -> p c h", h=16)
    RSZ = max(32 * TN * 2, 32 * TN * 2 + NS * D * 4)
    Roff = A.alloc(RSZ)
    rs = Sub(A, Roff, RSZ)
    qaT = rs.new(BF16, 8 * TN, pat="p (k t) -> p k t", t=TN)
    kaT = rs.new(BF16, 8 * TN, pat="p (k t) -> p k t", t=TN)
    vaT = rs.new(BF16, 16 * TN, pat="p (k t) -> p k t", t=TN)
    qbT = rs.new(BF16, 8 * TN, pat="p (k t) -> p k t", t=TN)
    kbT = rs.new(BF16, 8 * TN, pat="p (k t) -> p k t", t=TN)
    vbT = A.new(BF16, 16 * TN, pat="p (k t) -> p k t", t=TN)
    rs = Sub(A, Roff, RSZ)
    actT = rs.new(BF16, 32 * TN, pat="p (k t) -> p k t", t=TN)
    ysb = rs.new(F32, NS * D, pat="p (s d) -> p s d", d=D)
    NSET = 4
    PSZ = NSET * 4352
    Poff = A.alloc(PSZ)
    p_ = Sub(A, Poff, PSZ)
    xs = p_.new(BF16, D)
    junk = p_.new(BF16, D)
    p_ = Sub(A, Poff, PSZ)
    cpre = [p_.new(F32, TN + 4) for _ in range(NSET)]
    cacc = [p_.new(F32, TN) for _ in range(NSET)]
    csl = [p_.new(F32, TN) for _ in range(NSET)]
    crsl = [p_.new(F32, TN) for _ in range(NSET)]
    p_ = Sub(A, Poff, PSZ)
    rt = [p_.new(F32, TN) for _ in range(12)]
    q_ = A
    Ug = q_.new(F32, 512, parts=64, pat="p (h c) -> p h c", c=64)
    EL = q_.new(F32, 512, parts=64, pat="p (h c) -> p h c", c=64)
    EA = q_.new(F32, 512, parts=64, pat="p (h c) -> p h c", c=64)
    Mb = [q_.new(BF16, 512, parts=64, pat="p (h c) -> p h c", c=64) for _ in range(2)]
    Nb = [q_.new(BF16, 512, parts=64, pat="p (h c) -> p h c", c=64) for _ in range(2)]
    XTb = q_.new(BF16, 512, parts=64, pat="p (h c) -> p h c", c=64)
    gcl = q_.new(F32, 16, parts=64)
    XT = q_.new(F32, 512, parts=64, pat="p (h c) -> p h c", c=64)
    egc = q_.new(F32, 16, parts=64)
    bexp = q_.new(F32, 8, parts=64)
    eld = q_.new(F32, 8)
    _okp = A.alloc(2048)
    kp = A.view(_okp, BF16, 1024, parts=64, pat="p (h d) -> p h d", d=128)
    kdb = A.new(BF16, 1024, parts=64, pat="p (h d) -> p h d", d=256)
    kd = q_.new(BF16, 1024, parts=64, pat="p (h d) -> p h d", d=128)
    _ovb = A.alloc(4096)
    vbk = A.view(_ovb, BF16, 2048, parts=64, pat="p (h v) -> p h v", v=256)
    vtb = A.new(BF16, 2048, parts=64, pat="p (h v) -> p h v", v=512)
    vnew = q_.new(BF16, 2048, parts=64, pat="p (h v) -> p h v", v=256)
    nwT = q_.new(BF16, 512, pat="p (h c) -> p h c", c=64)
    egcr = q_.new(F32, 512, pat="p (h c) -> p h c", c=64)
    qeT = q_.new(BF16, 512, pat="p (h c) -> p h c", c=64)
    _oat = A.alloc(1024)
    attnT = A.view(_oat, BF16, 512, parts=64, pat="p (h c) -> p h c", c=64)
    scT = A.new(BF16, 256, parts=64, pat="p (h c) -> p h c", c=64)
    o_sb = q_.new(F32, 1024, pat="p (k c) -> p k c", c=64)
    o_sq = q_.new(BF16, 1024, pat="p (k c) -> p k c", c=64)
    onesb = q_.new(BF16, 128)
    o_rs = q_.new(F32, 512, pat="p (h c) -> p h c", c=64)
    o_sbB = q_.new(F32, 1024, pat="p (k c) -> p k c", c=64)
    o_sqB = q_.new(BF16, 1024, pat="p (k c) -> p k c", c=64)
    o_rsB = q_.new(F32, 256, pat="p (h c) -> p h c", c=64)
    print("SBUF arena used:", A.top, "of", ARENA, "R", RSZ)
    for _n, _v in (("oaT", oaT), ("obT", obT), ("mixT", mixT), ("xnT", xnT), ("Sa", Sa), ("x_tok", x_tok)):
        _lo = min(a[1] for a in _v.atoms) * ATOM
        DBG[_n] = (_lo, (max(a[1] for a in _v.atoms) + 1) * ATOM - _lo)

    sem_c = T.dsem("const")
    sem_x = T.dsem("x")
    sem_o = T.dsem("o")
    sem_w = [T.dsem(f"w{i}") for i in range(NSLAB)]
    sem_cs = T.dsem("cs")
    sem_cv = T.dsem("cv")

    def pbc(ap_row, n):
        return ap_row.partition_broadcast(n)

    T.dma([(c64v.ap, c64_d), (c128v.ap, c128_d),
           (npost_row.ap, npost_d.to_broadcast([128, D])),
           (mpost_row.ap, mpost_d.to_broadcast([128, D])),
           (dtb_row.ap, dtb_d.to_broadcast([64, 8])),
           (alog_row.ap, alog_d.to_broadcast([64, 8])),
           ], [], [c64v, c128v, npost_row, mpost_row, dtb_row, alog_row], sem_c)
    cstage = A.view(Roff, F32, 128)
    vstage = A.view(Roff + 1024, F32, 128, parts=34)
    T.dma([(cstage.ap, conv_d.rearrange("j (b p) -> (j b) p", p=128)),
           (vstage.ap[0:8, :], npre_d.rearrange("o (k p) -> (o k) p", p=128)),
           (vstage.ap[8:16, :], mpre_d.rearrange("o (k p) -> (o k) p", p=128)),
           (vstage.ap[16:18, :], na_d.rearrange("o (k p) -> (o k) p", p=128)),
           (vstage.ap[18:34, :], nb_d.rearrange("o (k p) -> (o k) p", p=128)),
           ] + [(wsm_f.ap[:, k, :], w_in_d[k * 128:(k + 1) * 128, BETA:BETA + 16]) for k in range(8)],
          [], [cstage, vstage, wsm_f], sem_c)
    b_ = PS()
    pc_ = psv(b_, F32, 128 + 34)
    T.mm(pc_[:, 0:128], cstage, identf, last=False, tr=True)
    T.mm(pc_[:, 128:162], vstage, View(identf.ap[0:34, 0:34], identf.atoms), tr=True)
    T.op("dve", lambda e: e.tensor_copy(cw.ap, pc_.ap[:, 0:128].rearrange("p (j b) -> p j b", j=4)), [pc_], [cw])
    T.op("dve", lambda e: e.tensor_copy(wpre.ap, pc_.ap[:, 128:136]), [pc_], [wpre])
    T.op("dve", lambda e: e.tensor_copy(wmlp.ap, pc_.ap[:, 136:144]), [pc_], [wmlp])
    T.op("dve", lambda e: e.tensor_copy(nav.ap, pc_.ap[:, 144:146]), [pc_], [nav])
    T.op("dve", lambda e: e.tensor_copy(nbv.ap, pc_.ap[:, 146:162]), [pc_], [nbv])
    NSTG = 3
    stg = [A.view(Roff + i * 8192, BF16, 4096) for i in range(NSTG)]
    sem_si = [T.dsem(f"si{i}") for i in range(NSTG)]
    sem_so = [T.dsem(f"so{i}") for i in range(NSTG)]
    wv = {}
    n_ = 0
    for src, dst, name in ((w_in_d, win_b, "in"), (w_bra_d, wbra_b, "bra"),
                           (w_brb_d, wbrb_b, "brb"), (w_out_d, wout_b, "out"),
                           (w_up_d, wup_b, "up"), (w_dn_d, wdn_b, "dn")):
        rows, cols = src.shape
        atoms = []
        for r in range(0, rows, 128):
            for c0 in range(0, cols, 4096):
                cw_ = min(4096, cols - c0)
                i = n_ % NSTG
                n_ += 1
                at = View(dst[r:r + 128, c0:c0 + cw_], (("dram", name, r, c0),))
                atoms.append(at.atoms[0])
                T.dma([(stg[i].ap[:, 0:cw_], src[r:r + 128, c0:c0 + cw_])], [], [stg[i]], sem_si[i], eng="pool")
                T.dma([(at.ap, stg[i].ap[:, 0:cw_])], [stg[i]], [at], sem_so[i])
        wv[name] = View(dst, tuple(atoms))

    T.op("dve", lambda e: e.tensor_copy(identb.ap, identf.ap), [identf], [identb])
    T.op("dve", lambda e: e.tensor_copy(wsm.ap, wsm_f.ap), [wsm_f], [wsm])
    T.op("dve", lambda e: e.tensor_copy(onesb.ap, onesf.ap), [onesf], [onesb])
    T.op("act", lambda e: e.activation(negA.ap, alog_row.ap, AF.Exp), [alog_row], [negA])
    T.op("dve", lambda e: e.tensor_scalar(negA.ap, negA.ap, -1.0, None, ALU.mult), [negA], [negA])
    T.op("pool", lambda e: e.memset(epsc.ap, EPS), [], [epsc])
    T.op("pool", lambda e: e.memset(halo.ap, 0.0), [], [halo])
    T.op("pool", lambda e: e.memset(Sa.ap, 0.0), [], [Sa])
    T.op("pool", lambda e: e.memset(Sab.ap, 0.0), [], [Sab])
    for i in range(2):
        T.op("pool", lambda e, i=i: e.memset(Sb[i].ap, 0.0), [], [Sb[i]])
        T.op("pool", lambda e, i=i: e.memset(Sbb[i].ap, 0.0), [], [Sbb[i]])

    slab_i = [0]

    def load_slab(wkey, dram_ap, buf=None):
        i = slab_i[0] if buf is None else buf
        slab_i[0] = (i + 1) % NSLAB
        free = 1
        for s_ in dram_ap.shape[1:]:
            free *= s_
        buf = wslab[i]
        dst = buf.ap[:, 0:free]
        if len(dram_ap.shape) == 3:
            dst = dst.rearrange("p (k c) -> p k c", c=dram_ap.shape[2])
        T.dma([(dst, dram_ap)], [wv[wkey]], [buf], sem_w[i])
        return View(dst, buf.atoms)

    def rms_to_T(dst, wvec):
        T.op("pool", lambda e: e.memset(ssq.ap, 0.0), [], [ssq])
        for s in range(NS):
            T.op("act", lambda e, s=s: e.activation(junk.ap, x_tok.ap[:, s, :], AF.Square,
                                                   accum_out=ssq.ap[:, s:s + 1]),
                 [x_tok], [junk, ssq])
        rsqrt(rstd, ssq[:, 0:NS], 1.0 / D)
        for s in range(NS):
            T.op("dve", lambda e, s=s: e.tensor_scalar(xs.ap, x_tok.ap[:, s, :], rstd.ap[:, s:s + 1], None,
                                                      ALU.mult), [x_tok, rstd], [xs])
            b = PS()
            pt = psv(b, BF16, 1024, pat="p (k t) -> p k t", t=128)
            for k in range(8):
                T.mm(pt[:, k, :], xs[:, k * 128:(k + 1) * 128], identb, last=(k == 7), tr=True)
            T.op("dve", lambda e, s=s, pt=pt: e.tensor_tensor(
                dst.ap[:, :, s * 128:(s + 1) * 128], pt.ap,
                wvec.ap.unsqueeze(2).to_broadcast([128, 8, 128]), ALU.mult), [pt, wvec], [dst])

    def proj_block(slab, kn, col, rhsT, M=128):
        b = PS()
        pt = psv(b, F32, TN, parts=M)
        for k in range(kn):
            T.mm(pt, slab[:, k, col:col + M], rhsT[:, k, :], start=(k == 0), stop=(k == kn - 1),
                 last=(k == kn - 1))
        return pt

    def win_slab(c0):
        return load_slab("in", win_b[:, c0:c0 + 512].rearrange("(k p) c -> p k c", p=128))

    def cut(tag):
        if upto != tag:
            return False
        T.dma([(out_d[0:TN, :].rearrange("(s p) d -> p s d", p=128), x_tok.ap)], [x_tok], [], sem_o)
        return True

    pools = [(0, 5, [0]), (5, 8, [5])]
    U_, UGT_, NEG1_ = k64("U"), k64("UGT"), k64("NEG1")
    NML_, NMUS_, NMUI_ = k64("NML"), k64("NMUS"), k64("NMUI")
    INTP_ = k64("INTP", HB * 64).re("p (h c) -> p h c", c=64)
    KDEC_ = k64("KDEC", HB)
    QDEC_ = k128("QDEC", HB * 64).re("p (h c) -> p h c", c=64)
    I64 = identf[0:64, 0:64]
    ones64 = onesf[0:64, 0:64]
    ones64x128 = onesf[0:64, 0:128]

    def b3(v, n=8):
        return v.un(1).bc([64, n, 64])

    for ti in range(NT):
        full = ti >= NT_WARM
        t0 = ti * TN
        T.dma([(x_tok.ap, x_d[t0:t0 + TN, :].rearrange("(s p) d -> p s d", p=128))], [], [x_tok], sem_x)
        T.dma([(cosv.ap, cos_d[:, t0:t0 + TN]), (sinv.ap, sin_d[:, t0:t0 + TN])], [], [cosv, sinv], sem_cs)
        if cut("pro"):
            break
        rms_to_T(xnT, wpre)
        if cut("s0"):
            break

        blocks = list(range(32)) if (full or ti == NT_WARM - 1) else list(range(8, 32))
        cur_slab, cur_sl = None, -1
        for n_, blk in enumerate(blocks):
            if blk // 4 != cur_sl:
                cur_sl = blk // 4
                cur_slab = win_slab(cur_sl * 512)
            pt = proj_block(cur_slab, 8, (blk % 4) * 128, xnT)
            j = n_ % NSET
            cp, ac, sl, crs = cpre[j], cacc[j], csl[j], crsl[j]
            sq = cp[:, 0:TN]
            T.op("pool", lambda e, cp=cp, blk=blk: e.tensor_copy(cp.ap[:, 0:3], halo.ap[:, blk, :]), [halo], [cp])
            T.op("act", lambda e, cp=cp, pt=pt: e.activation(cp.ap[:, 3:3 + TN], pt.ap, AF.Copy), [pt], [cp])
            T.op("pool", lambda e, cp=cp, blk=blk: e.tensor_copy(halo.ap[:, blk, :], cp.ap[:, TN:TN + 3]), [cp], [halo])
            T.op("dve", lambda e, cp=cp, ac=ac, blk=blk: e.tensor_scalar(
                ac.ap, cp.ap[:, 0:TN], cw.ap[:, 0, blk:blk + 1], None, ALU.mult), [cp, cw], [ac])
            T.op("dve", lambda e, cp=cp, ac=ac, blk=blk: e.scalar_tensor_tensor(
                ac.ap, cp.ap[:, 1:1 + TN], cw.ap[:, 1, blk:blk + 1], ac.ap, ALU.mult, ALU.add), [cp, cw, ac], [ac])
            for tap in (2, 3):
                T.op("dve", lambda e, cp=cp, ac=ac, blk=blk, tap=tap: e.scalar_tensor_tensor(
                    ac.ap, cp.ap[:, tap:tap + TN], cw.ap[:, tap, blk:blk + 1], ac.ap, ALU.mult, ALU.add),
                    [cp, cw, ac], [ac])
            if blk >= 16:
                dst = vaT[:, blk - 16, :]
                T.op("act", lambda e, ac=ac, dst=dst: e.activation(dst.ap, ac.ap, AF.Silu), [ac], [dst])
            else:
                T.op("act", lambda e, ac=ac, sl=sl: e.activation(sl.ap, ac.ap, AF.Silu), [ac], [sl])
                T.op("act", lambda e, sl=sl, sq=sq: e.activation(sq.ap, sl.ap, AF.Square), [sl], [sq])
                b = PS()
                p2 = psv(b, F32, TN)
                T.mm(p2, onesf, sq)
                rsqrt(crs, p2, 1.0)
                if blk < 8:
                    dst, scl = qaT[:, blk, :], DKA ** -0.5
                else:
                    dst, scl = kaT[:, blk - 8, :], 1.0
                T.op("dve", lambda e, sl=sl, dst=dst, scl=scl: e.scalar_tensor_tensor(
                    dst.ap, sl.ap, scl, crs.ap, ALU.mult, ALU.mult), [sl, crs], [dst])

        if cut("s1a"):
            break
        bg = PS()
        pg = psv(bg, F32, NCH * 16, parts=64, pat="p (c h) -> p c h", h=16)
        for c in range(NCH):
            for k in range(8):
                T.mm(pg[:, c, :], xnT[:, k, c * 64:(c + 1) * 64], wsm[:, k, :], start=(k == 0), stop=(k == 7),
                     last=(k == 7))
        T.op("dve", lambda e: e.tensor_copy(glog.ap, pg.ap), [pg], [glog])
        g_e1, g_l1, g_lnb, g_beta, g_t, g_g = (gsm[n] for n in ("e1", "l1", "lnb", "beta", "t", "g"))
        T.op("act", lambda e: e.activation(g_e1.ap, glog.ap[:, :, 0:8], AF.Exp, scale=-1.0), [glog], [g_e1])
        T.op("act", lambda e: e.activation(g_l1.ap, g_e1.ap, AF.Ln, bias=1.0), [g_e1], [g_l1])
        T.op("dve", lambda e: e.tensor_scalar(g_lnb.ap, g_l1.ap, -1.0, None, ALU.mult), [g_l1], [g_lnb])
        T.op("act", lambda e: e.activation(g_beta.ap, g_l1.ap, AF.Exp, scale=-1.0), [g_l1], [g_beta])
        T.op("dve", lambda e: e.tensor_tensor(g_t.ap, glog.ap[:, :, 8:16],
                                              dtb_row.ap.unsqueeze(1).to_broadcast([64, NCH, 8]), ALU.add),
             [glog, dtb_row], [g_t])
        T.op("act", lambda e: e.activation(g_t.ap, g_t.ap, AF.Exp), [g_t], [g_t])
        T.op("act", lambda e: e.activation(g_t.ap, g_t.ap, AF.Ln, bias=1.0), [g_t], [g_t])
        T.op("dve", lambda e: e.tensor_tensor(g_g.ap, g_t.ap, negA.ap.unsqueeze(1).to_broadcast([64, NCH, 8]),
                                              ALU.mult), [g_t, negA], [g_g])

        for which in ((0, 1) if full else (1,)):
            base = QB if which == 0 else KB
            dstT = qbT if which == 0 else kbT
            for h in range(HB):
                if h % 2 == 0:
                    cur_slab = win_slab(base + (h // 2) * 512)
                p1 = proj_block(cur_slab, 8, (h % 2) * 256, xnT)
                p2 = proj_block(cur_slab, 8, (h % 2) * 256 + 128, xnT)
                t1, t2, ta, tb, tc, td = rt[6 * (h % 2):6 * (h % 2) + 6]
                scl = 1.0 if which == 0 else DKB ** -0.5
                T.op("act", lambda e, p1=p1, scl=scl: e.activation(t1.ap, p1.ap, AF.Identity, scale=scl), [p1], [t1])
                T.op("act", lambda e, p2=p2, scl=scl: e.activation(t2.ap, p2.ap, AF.Identity, scale=scl), [p2], [t2])
                T.op("dve", lambda e: e.tensor_tensor(ta.ap, t1.ap, cosv.ap, ALU.mult), [t1, cosv], [ta])
                T.op("dve", lambda e: e.tensor_tensor(tb.ap, t2.ap, sinv.ap, ALU.mult), [t2, sinv], [tb])
                T.op("dve", lambda e, h=h, dstT=dstT: e.tensor_tensor(dstT.ap[:, 2 * h, :], ta.ap, tb.ap, ALU.subtract),
                     [ta, tb], [dstT])
                T.op("dve", lambda e: e.tensor_tensor(tc.ap, t1.ap, sinv.ap, ALU.mult), [t1, sinv], [tc])
                T.op("dve", lambda e: e.tensor_tensor(td.ap, t2.ap, cosv.ap, ALU.mult), [t2, cosv], [td])
                T.op("dve", lambda e, h=h, dstT=dstT: e.tensor_tensor(dstT.ap[:, 2 * h + 1, :], tc.ap, td.ap, ALU.add),
                     [tc, td], [dstT])
        cur_sl = -1
        for blk in range(16):
            if blk // 4 != cur_sl:
                cur_sl = blk // 4
                cur_slab = win_slab(VB + cur_sl * 512)
            pt = proj_block(cur_slab, 8, (blk % 4) * 128, xnT)
            if blk % 2:
                T.op("act", lambda e, pt=pt, blk=blk: e.activation(vbT.ap[:, blk, :], pt.ap, AF.Copy), [pt], [vbT])
            else:
                T.op("dve", lambda e, pt=pt, blk=blk: e.tensor_copy(vbT.ap[:, blk, :], pt.ap), [pt], [vbT])

        def a_chunk(c):
            cs = slice(c * 64, (c + 1) * 64)
            gc, lnbc, betac = g_g[:, c, :], g_lnb[:, c, :], g_beta[:, c, :]
            yield
            T.op("dve", lambda e, gc=gc: e.tensor_tensor(Ug.ap, b3(U_).ap, gc.ap.unsqueeze(2).to_broadcast([64, 8, 64]),
                                                         ALU.mult), [U_, gc], [Ug])
            b = PS()
            p_s = psv(b, F32, 16, parts=64)
            p_l = psv(b, F32, 512)[:, 256:264]
            T.mm(p_s[:, 0:8], U_, gc, last=False)
            T.mm(p_s[:, 8:16], UGT_, gc, last=False)
            T.mm(p_l, ones64x128, gc)
            bR = PS()
            pR = psv(bR, F32, 512, pat="p (h c) -> p h c", c=64)
            T.mm(View(pR.ap.rearrange("p h c -> p (h c)"), pR.atoms), ones64x128,
                 View(Ug.ap.rearrange("p h c -> p (h c)"), Ug.atoms))
            pR64 = View(pR.ap[0:64], pR.atoms)
            yield
            T.op("act", lambda e, p_s=p_s: e.activation(egc.ap, p_s.ap, AF.Exp), [p_s], [egc])
            yield
            T.op("act", lambda e, p_l=p_l: e.activation(eld.ap, p_l.ap, AF.Exp), [p_l], [eld])
            yield
            T.op("dve", lambda e, p_s=p_s, lnbc=lnbc: e.tensor_tensor(gcl.ap[:, 0:8], p_s.ap[:, 0:8], lnbc.ap, ALU.add),
                 [p_s, lnbc], [gcl])
            yield
            T.op("dve", lambda e, p_s=p_s: e.tensor_copy(gcl.ap[:, 8:16], p_s.ap[:, 0:8]), [p_s], [gcl])
            yield
            T.op("dve", lambda e, betac=betac: e.tensor_tensor(bexp.ap, betac.ap, egc.ap[:, 0:8], ALU.mult),
                 [betac, egc], [bexp])
            yield
            T.op("dve", lambda e, pR64=pR64: e.scalar_tensor_tensor(EL.ap, pR64.ap, -1.0, b3(NML_).ap, ALU.mult, ALU.add),
                 [pR64, NML_], [EL])
            yield
            T.op("dve", lambda e: e.tensor_tensor(EL.ap, EL.ap, gcl.ap[:, 0:8].unsqueeze(2).to_broadcast([64, 8, 64]),
                                                  ALU.add), [EL, gcl], [EL])
            yield
            T.op("act", lambda e: e.activation(EL.ap, EL.ap, AF.Exp), [EL], [EL])
            if full:
                yield
                T.op("dve", lambda e, pR64=pR64: e.tensor_tensor(EA.ap, pR64.ap, b3(NMUI_).ap, ALU.add), [pR64, NMUI_], [EA])
                yield
                T.op("dve", lambda e: e.tensor_tensor(EA.ap, EA.ap, gcl.ap[:, 8:16].unsqueeze(2).to_broadcast([64, 8, 64]),
                                                      ALU.subtract), [EA, gcl], [EA])
                yield
                T.op("act", lambda e: e.activation(EA.ap, EA.ap, AF.Exp), [EA], [EA])
                yield
                T.op("act", lambda e, pR=pR: e.activation(egcr.ap, pR.ap, AF.Exp), [pR], [egcr])
            bG = PS()
            pG = psv(bG, F32, 512, parts=64, pat="p (h c) -> p h c", c=64)
            for h in range(8):
                T.mm(pG[:, h, :], kaT[:, h, cs], kaT[:, h, cs], last=(h == 7))
            bK = PS()
            pK = psv(bK, BF16, 1024, parts=64, pat="p (h d) -> p h d", d=128)
            for h in range(8):
                T.mm(pK[:, h, :], kaT[:, h, cs], identb, last=(h == 7), tr=True)
            yield
            T.op("dve", lambda e, pK=pK: e.tensor_tensor(kp.ap, pK.ap, bexp.ap.unsqueeze(2).to_broadcast([64, 8, 128]),
                                                         ALU.mult), [pK, bexp], [kp])
            yield
            T.op("dve", lambda e, pK=pK: e.tensor_tensor(kd.ap, pK.ap,
                                                         egc.ap[:, 8:16].unsqueeze(2).to_broadcast([64, 8, 128]),
                                                         ALU.mult), [pK, egc], [kd])
            M, N = Mb[0], Nb[0]
            yield
            T.op("dve", lambda e, pG=pG, M=M: e.tensor_tensor(M.ap, pG.ap, EL.ap, ALU.mult), [pG, EL], [M])
            if full:
                yield
                T.op("dve", lambda e, cs=cs: e.tensor_tensor(qeT.ap, qaT.ap[:, :, cs], egcr.ap, ALU.mult),
                     [qaT, egcr], [qeT])
                bT = PS()
                pT = psv(bT, F32, 512, parts=64, pat="p (h c) -> p h c", c=64)
                for h in range(8):
                    T.mm(pT[:, h, :], kaT[:, h, cs], qaT[:, h, cs], last=(h == 7))
                yield
                T.op("dve", lambda e, pT=pT: e.tensor_tensor(attnT.ap, pT.ap, EA.ap, ALU.mult), [pT, EA], [attnT])

            bN0 = PS()
            pN0 = psv(bN0, BF16, 512, parts=64, pat="p (h c) -> p h c", c=64)
            for h in range(8):
                T.mm(pN0[:, h, :], M[:, h, :], View(identb.ap[0:64, 0:64], identb.atoms), last=(h == 7), tr=True)
            yield
            T.op("act", lambda e, pN0=pN0, N=N: e.activation(N.ap, pN0.ap, AF.Copy), [pN0], [N])
            yield
            T.op("dve", lambda e, pN0=pN0: e.tensor_tensor(XT.ap, b3(I64).ap, pN0.ap, ALU.subtract), [I64, pN0], [XT])
            yield
            T.op("dve", lambda e: e.tensor_copy(XTb.ap, XT.ap), [XT], [XTb])
            bM = PS()
            pM = psv(bM, F32, 512, parts=64, pat="p (h c) -> p h c", c=64)
            for h in range(8):
                T.mm(pM[:, h, :], N[:, h, :], M[:, h, :], last=(h == 7))
            bN = PS()
            pN = psv(bN, F32, 512, parts=64, pat="p (h c) -> p h c", c=64)
            for h in range(8):
                T.mm(pN[:, h, :], M[:, h, :], N[:, h, :], last=(h == 7))
            M, N = Mb[1], Nb[1]
            yield
            T.op("act", lambda e, pM=pM, M=M: e.activation(M.ap, pM.ap, AF.Copy), [pM], [M])
            yield
            T.op("dve", lambda e, pN=pN, N=N: e.tensor_copy(N.ap, pN.ap), [pN], [N])
            bV = PS(2)
            pV = psv(bV, BF16, 2048, parts=64, nb=2, pat="p (k d) -> p k d", d=128)
            for k in range(16):
                T.mm(pV[:, k, :], vaT[:, k, cs], identb, last=(k == 15), tr=True)
            yield
            T.op("dve", lambda e, pV=pV, betac=betac: e.tensor_tensor(
                vbk.ap, pV.ap.rearrange("p (h j) d -> p h (j d)", j=2),
                betac.ap.unsqueeze(2).to_broadcast([64, 8, 256]), ALU.mult), [pV, betac], [vbk])
            for k_ in range(1, 6):
                M2, N2 = Mb[(k_ + 1) % 2], Nb[(k_ + 1) % 2]
                if k_ < 5:
                    bM = PS()
                    pM = psv(bM, F32, 512, parts=64, pat="p (h c) -> p h c", c=64)
                    for h in range(8):
                        T.mm(pM[:, h, :], N[:, h, :], M[:, h, :], last=(h == 7))
                if k_ < 4:
                    bN = PS()
                    pN = psv(bN, F32, 512, parts=64, pat="p (h c) -> p h c", c=64)
                    for h in range(8):
                        T.mm(pN[:, h, :], M[:, h, :], N[:, h, :], last=(h == 7))
                bX = PS()
                pX = psv(bX, F32, 512, parts=64, pat="p (h c) -> p h c", c=64)
                for h in range(8):
                    T.mm(pX[:, h, :], M[:, h, :], XTb[:, h, :], last=(h == 7))
                if k_ < 5:
                    yield
                    T.op("act", lambda e, pM=pM, M2=M2: e.activation(M2.ap, pM.ap, AF.Copy), [pM], [M2])
                if k_ < 4:
                    yield
                    T.op("dve", lambda e, pN=pN, N2=N2: e.tensor_copy(N2.ap, pN.ap), [pN], [N2])
                yield
                T.op("dve", lambda e, pX=pX: e.tensor_tensor(XT.ap, XT.ap, pX.ap, ALU.add), [XT, pX], [XT])
                yield
                T.op("dve", lambda e: e.tensor_copy(XTb.ap, XT.ap), [XT], [XTb])
                M, N = M2, N2
            TinvT = XTb
            bW = PS()
            pW = psv(bW, F32, 512, pat="p (h c) -> p h c", c=64)
            for h in range(8):
                T.mm(pW[:, h, :], kp[:, h, :], TinvT[:, h, :], last=(h == 7))
            yield
            T.op("act", lambda e, pW=pW: e.activation(nwT.ap, pW.ap, AF.Identity, scale=-1.0), [pW], [nwT])
            for hh in range(2):
                bN_ = PS(2)
                pVN = psv(bN_, F32, 1024, parts=64, nb=2, pat="p (h v) -> p h v", v=256)
                for h4 in range(4):
                    h = hh * 4 + h4
                    T.mm(pVN[:, h4, :], TinvT[:, h, :], vbk[:, h, :], start=True, stop=False, last=False)
                    T.mm(pVN[:, h4, :], nwT[:, h, :], Sab[:, h, :], start=False, stop=True, last=(h4 == 3))
                yield
                T.op("act", lambda e, pVN=pVN, hh=hh: e.activation(vnew.ap[:, hh * 4:(hh + 1) * 4, :], pVN.ap, AF.Copy),
                     [pVN], [vnew])
            if full:
                bO = PS(2)
                pO = psv(bO, F32, 1024, nb=2, pat="p (k c) -> p k c", c=64)
                for h in range(8):
                    for j in range(2):
                        T.mm(pO[:, h * 2 + j, :], Sab[:, h, j * 128:(j + 1) * 128], qeT[:, h, :],
                             start=True, stop=False, last=False)
                        T.mm(pO[:, h * 2 + j, :], vnew[:, h, j * 128:(j + 1) * 128], attnT[:, h, :],
                             start=False, stop=True, last=(h == 7 and j == 1))
                yield
                T.op("act", lambda e, pO=pO: e.activation(o_sb.ap, pO.ap, AF.Copy), [pO], [o_sb])
                yield
                T.op("act", lambda e: e.activation(o_sq.ap, o_sb.ap, AF.Square), [o_sb], [o_sq])
                bS = PS()
                pS = psv(bS, F32, 512, pat="p (h c) -> p h c", c=64)
                for h in range(8):
                    for j in range(2):
                        T.mm(pS[:, h, :], onesb, o_sq[:, h * 2 + j, :], start=(j == 0), stop=(j == 1),
                             last=(h == 7 and j == 1))
                yield
                rsqrt(o_rs, pS, 1.0 / DVA)
                yield
                T.op("dve", lambda e: e.tensor_tensor(
                    o_sb.ap.rearrange("p (h j) c -> p h j c", j=2), o_sb.ap.rearrange("p (h j) c -> p h j c", j=2),
                    o_rs.ap.unsqueeze(2).to_broadcast([128, 8, 2, 64]), ALU.mult), [o_sb, o_rs], [o_sb])
                yield
                T.op("dve", lambda e, cs=cs: e.tensor_tensor(
                    oaT.ap[:, :, cs].rearrange("p (h j) c -> p h j c", j=2),
                    o_sb.ap.rearrange("p (h j) c -> p h j c", j=2),
                    nav.ap.unsqueeze(1).unsqueeze(3).to_broadcast([128, 8, 2, 64]), ALU.mult), [o_sb, nav], [oaT])
            for hh in range(2):
                bS2 = PS(2)
                pS2 = psv(bS2, F32, 1024, nb=2, pat="p (h v) -> p h v", v=256)
                for h4 in range(4):
                    h = hh * 4 + h4
                    T.mm(pS2[:, h4, :], kd[:, h, :], vnew[:, h, :], last=(h4 == 3))
                for h4 in range(4):
                    h = hh * 4 + h4
                    yield
                    T.op("dve", lambda e, h=h, h4=h4, pS2=pS2: e.scalar_tensor_tensor(
                        Sa.ap[:, h, :], Sa.ap[:, h, :], eld.ap[:, h:h + 1], pS2.ap[:, h4, :], ALU.mult, ALU.add),
                        [Sa, eld, pS2], [Sa])
            yield
            T.op("act", lambda e: e.activation(Sab.ap, Sa.ap, AF.Copy), [Sa], [Sab])

            yield
        def b_chunk(c):
            cs = slice(c * 64, (c + 1) * 64)
            bK = PS()
            pK = psv(bK, BF16, 1024, parts=64, pat="p (h d) -> p h d", d=256)
            for k in range(8):
                T.mm(pK[:, k // 2, (k % 2) * 128:(k % 2 + 1) * 128], kbT[:, k, cs], identb, last=(k == 7), tr=True)
            yield
            T.op("dve", lambda e, pK=pK: e.tensor_tensor(kdb.ap, pK.ap, KDEC_.ap.unsqueeze(2).to_broadcast([64, 4, 256]),
                                                         ALU.mult), [pK, KDEC_], [kdb])
            bV = PS(2)
            pV = psv(bV, BF16, 2048, parts=64, nb=2, pat="p (h v) -> p h v", v=512)
            for k in range(16):
                T.mm(pV[:, k // 4, (k % 4) * 128:(k % 4 + 1) * 128], vbT[:, k, cs], identb, last=(k == 15), tr=True)
            yield
            T.op("act", lambda e, pV=pV: e.activation(vtb.ap, pV.ap, AF.Copy), [pV], [vtb])
            if full:
                bC = PS()
                pC = psv(bC, F32, 256, parts=64, pat="p (h c) -> p h c", c=64)
                for h in range(HB):
                    T.mm(pC[:, h, :], kbT[:, 2 * h, cs], qbT[:, 2 * h, cs], start=True, stop=False, last=False)
                    T.mm(pC[:, h, :], kbT[:, 2 * h + 1, cs], qbT[:, 2 * h + 1, cs], start=False, stop=True,
                         last=(h == HB - 1))
                yield
                T.op("dve", lambda e, pC=pC: e.tensor_tensor(scT.ap, pC.ap, INTP_.ap, ALU.mult), [pC, INTP_], [scT])
                bO = PS(2)
                pO = psv(bO, F32, 1024, nb=2, pat="p (k c) -> p k c", c=64)
                for h in range(HB):
                    for j in range(4):
                        o_ = pO[:, h * 4 + j, :]
                        T.mm(o_, Sbb[0][:, h, j * 128:(j + 1) * 128], qbT[:, 2 * h, cs], start=True, stop=False, last=False)
                        T.mm(o_, Sbb[1][:, h, j * 128:(j + 1) * 128], qbT[:, 2 * h + 1, cs], start=False, stop=False,
                             last=False)
                        T.mm(o_, vtb[:, h, j * 128:(j + 1) * 128], scT[:, h, :], start=False, stop=True,
                             last=(h == HB - 1 and j == 3))
                yield
                T.op("dve", lambda e, pO=pO: e.tensor_tensor(
                    o_sbB.ap.rearrange("p (h j) c -> p h j c", j=4), pO.ap.rearrange("p (h j) c -> p h j c", j=4),
                    QDEC_.ap.unsqueeze(2).to_broadcast([128, 4, 4, 64]), ALU.mult), [pO, QDEC_], [o_sbB])
                yield
                T.op("act", lambda e: e.activation(o_sqB.ap, o_sbB.ap, AF.Square), [o_sbB], [o_sqB])
                bS = PS()
                pS = psv(bS, F32, 256, pat="p (h c) -> p h c", c=64)
                for h in range(HB):
                    for j in range(4):
                        T.mm(pS[:, h, :], onesb, o_sqB[:, h * 4 + j, :], start=(j == 0), stop=(j == 3),
                             last=(h == HB - 1 and j == 3))
                yield
                rsqrt(o_rsB, pS, 1.0 / DVB)
                yield
                T.op("dve", lambda e: e.tensor_tensor(
                    o_sbB.ap.rearrange("p (h j) c -> p h j c", j=4), o_sbB.ap.rearrange("p (h j) c -> p h j c", j=4),
                    o_rsB.ap.unsqueeze(2).to_broadcast([128, 4, 4, 64]), ALU.mult), [o_sbB, o_rsB], [o_sbB])
                yield
                T.op("dve", lambda e, cs=cs: e.tensor_tensor(
                    obT.ap[:, :, cs], o_sbB.ap, nbv.ap.unsqueeze(2).to_broadcast([128, 16, 64]), ALU.mult),
                    [o_sbB, nbv], [obT])
            for h in range(HB):
                bS2 = PS(2)
                for d_ in range(2):
                    pS2 = psv(bS2 + d_, F32, 512)
                    T.mm(pS2, kdb[:, h, d_ * 128:(d_ + 1) * 128], vtb[:, h, :])
                    yield
                    T.op("dve", lambda e, h=h, d_=d_, pS2=pS2: e.scalar_tensor_tensor(
                        Sb[d_].ap[:, h, :], Sb[d_].ap[:, h, :], cdec[h], pS2.ap, ALU.mult, ALU.add),
                        [Sb[d_], pS2], [Sb[d_]])
            yield
            T.op("act", lambda e: e.activation(Sbb[1].ap, Sb[1].ap, AF.Copy), [Sb[1]], [Sbb[1]])
            yield
            T.op("act", lambda e: e.activation(Sbb[0].ap, Sb[0].ap, AF.Copy), [Sb[0]], [Sbb[0]])

            yield
        for c in range(NCH):
            gens = [a_chunk(c), b_chunk(c)]
            alive = [True, True]
            while any(alive):
                for gi, reps in ((0, 2), (1, 1)):
                    for _ in range(reps):
                        if alive[gi]:
                            ps_pool[0] = pools[gi]
                            try:
                                next(gens[gi])
                            except StopIteration:
                                alive[gi] = False
            ps_pool[0] = None
        if full:
            cur_sl = -1
            for blk in range(16):
                if blk // 4 != cur_sl:
                    cur_sl = blk // 4
                    cur_slab = win_slab(ZA + cur_sl * 512)
                pt = proj_block(cur_slab, 8, (blk % 4) * 128, xnT)
                sz = rt[blk % 4]
                T.op("act", lambda e, pt=pt, sz=sz: e.activation(sz.ap, pt.ap, AF.Silu), [pt], [sz])
                T.op("dve", lambda e, blk=blk, sz=sz: e.tensor_tensor(
                    oaT.ap[:, blk, :], oaT.ap[:, blk, :], sz.ap, ALU.mult), [oaT, sz], [oaT])

        if cut("a"):
            break
        if cut("b"):
            break
        if not full:
            continue

        cur_sl = -1
        for blk in range(16):
            if blk // 4 != cur_sl:
                cur_sl = blk // 4
                cur_slab = win_slab(GB + cur_sl * 512)
            pt = proj_block(cur_slab, 8, (blk % 4) * 128, xnT)
            sz = rt[blk % 4]
            T.op("act", lambda e, pt=pt, sz=sz: e.activation(sz.ap, pt.ap, AF.Silu), [pt], [sz])
            T.op("dve", lambda e, blk=blk, sz=sz: e.tensor_tensor(
                obT.ap[:, blk, :], obT.ap[:, blk, :], sz.ap, ALU.mult), [obT, sz], [obT])

        for br in range(2):
            wk, wd, oT, gbase = (("bra", wbra_b, oaT, GTA), ("brb", wbrb_b, obT, GTB))[br]
            for cb in range(8):
                if cb % 2 == 0:
                    sa_ = load_slab(wk, wd[:, cb * 128:cb * 128 + 256].rearrange("(k p) c -> p k c", p=128), buf=0)
                if cb % 4 == 0:
                    sg_ = load_slab("in", win_b[:, gbase + (cb // 4) * 512:gbase + (cb // 4) * 512 + 512]
                                    .rearrange("(k p) c -> p k c", p=128), buf=1)
                pa = proj_block(sa_, 16, (cb % 2) * 128, oT)
                pg_ = proj_block(sg_, 8, (cb % 4) * 128, xnT)
                T.op("act", lambda e, pg_=pg_: e.activation(rt[2 * (cb % 4)].ap, pg_.ap, AF.Sigmoid), [pg_], [rt[2 * (cb % 4)]])
                if br == 0:
                    T.op("dve", lambda e, pa=pa, cb=cb: e.tensor_tensor(mixT.ap[:, cb, :], pa.ap, rt[2 * (cb % 4)].ap, ALU.mult),
                         [pa, rt[2 * (cb % 4)]], [mixT])
                else:
                    T.op("dve", lambda e, pa=pa: e.tensor_tensor(rt[2 * (cb % 4) + 1].ap, pa.ap, rt[2 * (cb % 4)].ap, ALU.mult), [pa, rt[2 * (cb % 4)]], [rt[2 * (cb % 4) + 1]])
                    T.op("dve", lambda e, cb=cb: e.tensor_tensor(mixT.ap[:, cb, :], mixT.ap[:, cb, :], rt[2 * (cb % 4) + 1].ap, ALU.add),
                         [rt[2 * (cb % 4) + 1], mixT], [mixT])

        def tok_proj_norm_res(srcT, kn, wkey, wdram, post_row):
            T.op("pool", lambda e: e.memset(ssq.ap, 0.0), [], [ssq])
            KG = 8
            for half in range(2):
                pts = [psv(PS(), F32, 512) for _ in range(NS)]
                for kg in range(kn // KG):
                    sl_ = load_slab(wkey, wdram[kg * KG * 128:(kg + 1) * KG * 128, half * 512:(half + 1) * 512]
                                    .rearrange("(k p) c -> p k c", p=128))
                    for s in range(NS):
                        for k in range(KG):
                            kk = kg * KG + k
                            T.mm(pts[s], srcT[:, kk, s * 128:(s + 1) * 128], sl_[:, k, :], start=(kk == 0),
                                 stop=(kk == kn - 1), last=(k == KG - 1))
                for s in range(NS):
                    pt = pts[s]
                    T.op("dve", lambda e, pt=pt, half=half, s=s: e.tensor_copy(
                        ysb.ap[:, s, half * 512:(half + 1) * 512], pt.ap), [pt], [ysb])
                    T.op("act", lambda e, half=half, s=s: e.activation(
                        junk.ap[:, 0:512], ysb.ap[:, s, half * 512:(half + 1) * 512], AF.Square,
                        accum_out=ssq.ap[:, 2 * s + half:2 * s + half + 1]), [ysb], [junk, ssq])
            for s in range(NS):
                T.op("dve", lambda e, s=s: e.tensor_tensor(rstd.ap[:, s:s + 1], ssq.ap[:, 2 * s:2 * s + 1],
                                                           ssq.ap[:, 2 * s + 1:2 * s + 2], ALU.add), [ssq], [rstd])
                rsqrt(rstd[:, s:s + 1], rstd[:, s:s + 1], 1.0 / D)
                T.op("dve", lambda e, s=s: e.scalar_tensor_tensor(ysb.ap[:, s, :], ysb.ap[:, s, :], rstd.ap[:, s:s + 1],
                                                                  post_row.ap, ALU.mult, ALU.mult),
                     [ysb, rstd, post_row], [ysb])
                T.op("dve", lambda e, s=s: e.tensor_tensor(x_tok.ap[:, s, :], x_tok.ap[:, s, :], ysb.ap[:, s, :],
                                                            ALU.add), [x_tok, ysb], [x_tok])

        tok_proj_norm_res(mixT, 8, "out", wout_b, npost_row)
        rms_to_T(xnT, wmlp)
        for fb in range(32):
            if fb % 4 == 0:
                su_ = load_slab("up", wup_b[:, fb * 128:fb * 128 + 512].rearrange("(k p) c -> p k c", p=128))
            pt = proj_block(su_, 8, (fb % 4) * 128, xnT)
            r_ = rt[fb % 4]
            T.op("act", lambda e, pt=pt, r_=r_: e.activation(r_.ap, pt.ap, AF.Relu), [pt], [r_])
            T.op("dve", lambda e, fb=fb, r_=r_: e.tensor_tensor(
                actT.ap[:, fb, :], r_.ap, r_.ap, ALU.mult), [r_], [actT])
        tok_proj_norm_res(actT, 32, "dn", wdn_b, mpost_row)
        o0 = (ti - NT_WARM) * TN
        T.dma([(out_d[o0:o0 + TN, :].rearrange("(s p) d -> p s d", p=128), x_tok.ap)], [x_tok], [], sem_o)

    T.wait_all("sp", [sem_o])
    print("instructions:", T.n_inst, "waits:", T.n_wait)
    return nc


_CACHE = {}


def _program(NT, NT_WARM, TN):
    key = (NT, NT_WARM, TN)
    if key not in _CACHE:
        _CACHE[key] = build(NT, NT_WARM, TN)
    return _CACHE[key]


def kernel(x, norm_mix_pre, norm_mix_post, norm_mlp_pre, norm_mlp_post, w_in, conv_a, a_log,
           dt_bias, norm_a, norm_b, w_br_a, w_br_b, w_out, w_up, w_down, _tn=256):
    x = np.asarray(x, np.float32)
    B, S, _ = x.shape
    TN = _tn
    half = S // 2
    NT_WARM = half // TN
    NT = 2 * NT_WARM
    a64, off64, a128, off128, cdec = host_consts()
    f = lambda a: np.ascontiguousarray(np.asarray(a, np.float32))
    shared = {
        "w_in": f(w_in[0]), "w_br_a": f(w_br_a[0]), "w_br_b": f(w_br_b[0]), "w_out": f(w_out[0]),
        "w_up": f(w_up[0]), "w_down": f(w_down[0]), "conv_a": f(conv_a[0]), "a_log": f(a_log),
        "dt_bias": f(dt_bias), "norm_a": f(norm_a), "norm_b": f(norm_b), "norm_mix_pre": f(norm_mix_pre),
        "norm_mix_post": f(norm_mix_post), "norm_mlp_pre": f(norm_mlp_pre), "norm_mlp_post": f(norm_mlp_post),
        "c64": a64, "c128": a128,
    }
    in_maps = []
    for b in range(B):
        for hf in range(2):
            if hf == 0:
                xs = np.concatenate([np.zeros((half, D), np.float32), x[b, :half]], axis=0)
            else:
                xs = x[b]
            rc, rs_ = rope_tables((hf - 1) * half, S)
            m = dict(shared)
            m["x"] = np.ascontiguousarray(xs)
            m["rcos"], m["rsin"] = rc, rs_
            in_maps.append(m)
    nc = _program(NT, NT_WARM, TN)
    res = run_bass_kernel_spmd(nc, in_maps, core_ids=list(range(len(in_maps))))
    out = np.empty((B, S, D), np.float32)
    for b in range(B):
        for hf in range(2):
            out[b, hf * half:(hf + 1) * half] = res.results[b * 2 + hf]["out"]
    return out
```

```python
import math
import numpy as np
import concourse.bass as bass
import concourse.mybir as mybir
from concourse.bass_utils import run_bass_kernel_spmd

F32 = mybir.dt.float32
BF16 = mybir.dt.bfloat16
U8 = mybir.dt.uint8
AF = mybir.ActivationFunctionType
ALU = mybir.AluOpType

D = 1024
EPS = 1e-6
C = 64
HA, DKA, DVA = 8, 128, 256
HB, DKB, DVB = 4, 256, 512
DFF = 4096
QA, KA, VA, ZA, BETA, DTC, QB, KB, VB, GB, GTA, GTB = (
    0, 1024, 2048, 4096, 6144, 6152, 6160, 7184, 8208, 10256, 12304, 13328)
DIN = 14352
NEG = -30000.0
ATOM = 256
ARENA = 212736

SAME_SYNC = False
NOSYNC_ENGINES = ("dve",)
DBG = {}


class View:
    __slots__ = ("ap", "atoms")

    def __init__(self, ap, atoms):
        self.ap = ap
        self.atoms = atoms

    def __getitem__(self, key):
        return View(self.ap[key], self.atoms)

    def re(self, s, **kw):
        return View(self.ap.rearrange(s, **kw), self.atoms)

    def un(self, axis):
        return View(self.ap.unsqueeze(axis), self.atoms)

    def bc(self, shape):
        return View(self.ap.to_broadcast(list(shape)), self.atoms)


class Eng:
    def __init__(self, name, h, sem):
        self.name, self.h, self.sem = name, h, sem
        self.cnt = 0
        self.seen = {}


class DSem:
    def __init__(self, h, key):
        self.h, self.key, self.val = h, key, 0


class Trk:
    def __init__(self, nc):
        self.nc = nc
        self.sems = {}
        self.eng = {}
        for name, h in (("pe", nc.tensor), ("dve", nc.vector), ("act", nc.scalar),
                        ("pool", nc.gpsimd)):
            s = nc.alloc_semaphore(name="sem_" + name)
            self.sems[name] = s
            self.eng[name] = Eng(name, h, s)
        self.sp = Eng("sp", nc.sync, None)
        self.eng["sp"] = self.sp
        self.dsems = {}
        self.st = {}
        self.pend_r = []
        self.pend_w = []
        self.n_wait = 0
        self.n_inst = 0

    def dsem(self, name):
        h = self.nc.alloc_semaphore(name="dsem_" + name)
        self.sems["d_" + name] = h
        d = DSem(h, "d_" + name)
        self.dsems["d_" + name] = d
        return d

    def _needs(self, reads, writes):
        need = {}
        for v in reads:
            for a in v.atoms:
                s = self.st.get(a)
                if s:
                    for k, x in s[0].items():
                        if need.get(k, 0) < x:
                            need[k] = x
        for v in writes:
            for a in v.atoms:
                s = self.st.get(a)
                if s:
                    for k, x in s[0].items():
                        if need.get(k, 0) < x:
                            need[k] = x
                    for k, x in s[1].items():
                        if need.get(k, 0) < x:
                            need[k] = x
        return need

    def _sync(self, eng, reads, writes):
        need = self._needs(reads, writes)
        for v in reads:
            for a in v.atoms:
                if a[0] == "ps":
                    s = self.st.get(a)
                    if s:
                        for k, x in s[1].items():
                            if k != eng.name and need.get(k, 0) < x:
                                need[k] = x
        for k, x in need.items():
            if k == eng.name and (eng.name == "pe" or (not SAME_SYNC and eng.name in NOSYNC_ENGINES)):
                continue
            if k in self.dsems:
                x = self.dsems[k].val
            if eng.seen.get(k, 0) >= x:
                continue
            eng.h.wait_ge(self.sems[k], x)
            eng.seen[k] = x
            self.n_wait += 1

    def _record(self, key, val, reads, writes):
        for v in reads:
            for a in v.atoms:
                s = self.st.setdefault(a, [{}, {}])
                if s[1].get(key, 0) < val:
                    s[1][key] = val
        for v in writes:
            for a in v.atoms:
                self.st[a] = [{key: val}, {}]

    def op(self, e, fn, reads, writes):
        eng = self.eng[e]
        self._sync(eng, reads, writes)
        inst = fn(eng.h)
        eng.cnt += 1
        inst.then_inc(eng.sem, 1)
        self._record(eng.name, eng.cnt, reads, writes)
        self.n_inst += 1

    def mm(self, out, lhsT, rhs, start=True, stop=True, last=True, tr=False):
        eng = self.eng["pe"]
        reads, writes = [lhsT, rhs], [out]
        self._sync(eng, reads, writes)
        if tr:
            inst = eng.h.transpose(out.ap, lhsT.ap, rhs.ap)
        else:
            inst = eng.h.matmul(out.ap, lhsT.ap, rhs.ap, start=start, stop=stop)
        self.pend_r += reads
        self.pend_w += writes
        self.n_inst += 1
        if last:
            eng.cnt += 1
            inst.then_inc(eng.sem, 1)
            self._record("pe", eng.cnt, self.pend_r, self.pend_w)
            self.pend_r, self.pend_w = [], []

    def dma(self, pairs, reads, writes, sem, eng="sp", **kw):
        e = self.eng[eng]
        self._sync(e, reads, writes)
        for o, i in pairs:
            inst = e.h.dma_start(out=o, in_=i, **kw)
            sem.val += 16
            inst.then_inc(sem.h, 16)
            self.n_inst += 1
        self._record(sem.key, sem.val, reads, writes)

    def wait_all(self, eng, sems):
        e = self.eng[eng]
        for s in sems:
            if s.val > 0 and e.seen.get(s.key, 0) < s.val:
                e.h.wait_ge(s.h, s.val)
                e.seen[s.key] = s.val


class Arena:
    def __init__(self, nc, nbytes):
        self.cm = nc.sbuf_tensor("arena", [128, nbytes], U8)
        self.t = self.cm.__enter__()
        self.n = nbytes
        self.top = 0

    def alloc(self, nbytes):
        nbytes = (nbytes + ATOM - 1) // ATOM * ATOM
        off = self.top
        self.top += nbytes
        assert self.top <= self.n, f"SBUF arena overflow {self.top} > {self.n}"
        return off

    def view(self, off, dtype, free, parts=128, pat=None, **kw):
        esz = 2 if dtype == BF16 else 4
        nb = free * esz
        self.last = (off, nb)
        ap = self.t[0:parts, off:off + nb].bitcast(dtype)
        if pat:
            ap = ap.rearrange(pat, **kw)
        atoms = tuple(("sb", i) for i in range(off // ATOM, (off + nb - 1) // ATOM + 1))
        return View(ap, atoms)

    def new(self, dtype, free, parts=128, pat=None, **kw):
        esz = 2 if dtype == BF16 else 4
        off = self.alloc(free * esz)
        return self.view(off, dtype, free, parts, pat, **kw)


class Sub:
    def __init__(self, arena, off, size):
        self.a, self.off, self.size, self.top = arena, off, size, 0

    def new(self, dtype, free, parts=128, pat=None, **kw):
        esz = 2 if dtype == BF16 else 4
        nb = (free * esz + ATOM - 1) // ATOM * ATOM
        o = self.off + self.top
        self.top += nb
        assert self.top <= self.size, f"sub-region overflow {self.top} > {self.size}"
        return self.a.view(o, dtype, free, parts, pat, **kw)


def host_consts():
    c64 = {}
    i = np.arange(64)
    c64["U"] = (i[:, None] <= i[None, :]).astype(np.float32)
    c64["UGT"] = (i[:, None] > i[None, :]).astype(np.float32)
    c64["NEG1"] = -np.ones((64, 64), np.float32)
    c64["NML"] = np.where(i[:, None] > i[None, :], 0.0, NEG).astype(np.float32)
    c64["NMUS"] = np.where(i[None, :] > i[:, None], 0.0, NEG).astype(np.float32)
    c64["NMUI"] = np.where(i[None, :] >= i[:, None], 0.0, NEG).astype(np.float32)
    h = np.arange(HB, dtype=np.float64)
    lg = np.log1p(-np.exp2(-5.0 - h))
    pos = np.arange(64, dtype=np.float64)
    dist = np.abs(pos[:, None] - pos[None, :])
    qdec = np.exp(lg[:, None] * (pos + 1.0))
    kdec = np.exp(lg[:, None] * (63.0 - pos))
    intra = np.exp(lg[:, None, None] * dist)
    intp = intra / qdec[:, None, :]
    c64["INTP"] = np.transpose(intp, (1, 0, 2)).reshape(64, HB * 64).astype(np.float32)
    c64["KDEC"] = kdec.T.astype(np.float32).copy()
    names64 = ["U", "UGT", "NEG1", "NML", "NMUS", "NMUI", "INTP", "KDEC"]
    off64, cols = {}, 0
    for n in names64:
        off64[n] = cols
        cols += c64[n].shape[1]
    a64 = np.concatenate([c64[n] for n in names64], axis=1)
    c128 = {}
    c128["IDF"] = np.eye(128, dtype=np.float32)
    c128["ONES"] = np.ones((128, 128), np.float32)
    c128["QDEC"] = np.broadcast_to(qdec.reshape(1, HB * 64), (128, HB * 64)).astype(np.float32)
    names128 = ["IDF", "ONES", "QDEC"]
    off128, cols = {}, 0
    for n in names128:
        off128[n] = cols
        cols += c128[n].shape[1]
    a128 = np.concatenate([c128[n] for n in names128], axis=1)
    cdec = [float(np.exp(lg[k] * 64.0)) for k in range(HB)]
    return a64, off64, a128, off128, cdec


def rope_tables(pos0, n):
    inv = 10000.0 ** (-(np.arange(0, DKB, 2, dtype=np.float32) / np.float32(DKB)))
    inv = inv.astype(np.float32)
    p = (pos0 + np.arange(n)).astype(np.float32)
    ang = (p[None, :] * inv[:, None]).astype(np.float32)
    return np.cos(ang.astype(np.float64)).astype(np.float32), np.sin(ang.astype(np.float64)).astype(np.float32)


def build(NT, NT_WARM, TN=256, upto=None):
    NS, NCH = TN // 128, TN // C
    NTOK = NT * TN
    NOUT = (NT - NT_WARM) * TN
    a64, off64, a128, off128, cdec = host_consts()
    nc = bass.Bass("TRN2", target_bir_lowering=False)

    def din(name, shape, dt=F32):
        return nc.dram_tensor(name, list(shape), dt, kind="ExternalInput").ap()

    x_d = din("x", [NTOK, D])
    w_in_d = din("w_in", [D, DIN])
    w_bra_d = din("w_br_a", [2048, D])
    w_brb_d = din("w_br_b", [2048, D])
    w_out_d = din("w_out", [D, D])
    w_up_d = din("w_up", [D, DFF])
    w_dn_d = din("w_down", [DFF, D])
    conv_d = din("conv_a", [4, 4096])
    alog_d = din("a_log", [1, 8])
    dtb_d = din("dt_bias", [1, 8])
    na_d = din("norm_a", [1, 256])
    nb_d = din("norm_b", [1, 2048])
    npre_d = din("norm_mix_pre", [1, D])
    npost_d = din("norm_mix_post", [1, D])
    mpre_d = din("norm_mlp_pre", [1, D])
    mpost_d = din("norm_mlp_post", [1, D])
    c64_d = din("c64", a64.shape)
    c128_d = din("c128", a128.shape)
    cos_d = din("rcos", [128, NTOK])
    sin_d = din("rsin", [128, NTOK])
    out_d = nc.dram_tensor("out", [NOUT, D], F32, kind="ExternalOutput").ap()

    def dscr(name, shape):
        return nc.dram_tensor(name, list(shape), BF16, kind="Internal").ap()

    win_b = dscr("win_b", [D, DIN])
    wbra_b = dscr("wbra_b", [2048, D])
    wbrb_b = dscr("wbrb_b", [2048, D])
    wout_b = dscr("wout_b", [D, D])
    wup_b = dscr("wup_b", [D, DFF])
    wdn_b = dscr("wdn_b", [DFF, D])

    T = Trk(nc)
    A = Arena(nc, ARENA)
    ps_cm = nc.psum_tensor("psum", [128, 4096], F32)
    ps_t = ps_cm.__enter__()
    ps_ptr = [0]

    ps_pool = [None]

    def PS(nb=1):
        if ps_pool[0] is None:
            lo, hi, ptr = 0, 8, ps_ptr
        else:
            lo, hi, ptr = ps_pool[0]
        b = ptr[0]
        if b < lo or b + nb > hi:
            b = lo
        ptr[0] = b + nb
        return b

    def psv(bank, dtype, free, parts=128, nb=1, pat=None, **kw):
        ap = ps_t[0:parts, bank * 512:(bank + nb) * 512]
        if dtype == BF16:
            ap = ap.bitcast(BF16)
        ap = ap[:, 0:free]
        if pat:
            ap = ap.rearrange(pat, **kw)
        return View(ap, tuple(("ps", bank + j) for j in range(nb)))

    def dv(ap, name):
        return View(ap, (("dram", name),))

    c64v = A.new(F32, a64.shape[1], parts=64)
    c128v = A.new(F32, a128.shape[1])

    def k64(n, w=64):
        return c64v[:, off64[n]:off64[n] + w]

    def k128(n, w=128):
        return c128v[:, off128[n]:off128[n] + w]

    identf = k128("IDF")
    onesf = k128("ONES")
    identb = A.new(BF16, 128)
    epsc = A.new(F32, 1)

    def rsqrt(dst, src, mult):
        P = dst.ap.shape[0]
        T.op("act", lambda e: e.activation(dst.ap, src.ap, AF.Ln, bias=epsc.ap[0:P, :], scale=mult), [src, epsc], [dst])
        T.op("act", lambda e: e.activation(dst.ap, dst.ap, AF.Exp, scale=-0.5), [dst], [dst])

    cw = A.new(F32, 32 * 4, pat="p (j b) -> p j b", j=4)
    wpre = A.new(F32, 8)
    wmlp = A.new(F32, 8)
    nav = A.new(F32, 2)
    nbv = A.new(F32, 16)
    npost_row = A.new(F32, D)
    mpost_row = A.new(F32, D)
    dtb_row = A.new(F32, 8, parts=64)
    alog_row = A.new(F32, 8, parts=64)
    negA = A.new(F32, 8, parts=64)
    wsm_f = A.new(F32, 8 * 16, pat="p (k c) -> p k c", c=16)
    wsm = A.new(BF16, 8 * 16, pat="p (k c) -> p k c", c=16)
    halo = A.new(F32, 32 * 3, pat="p (b j) -> p b j", j=3)
    Sa = A.new(F32, HA * DVA, pat="p (h v) -> p h v", v=DVA)
    Sab = A.new(BF16, HA * DVA, pat="p (h v) -> p h v", v=DVA)
    Sb = [A.new(F32, HB * DVB, pat="p (h v) -> p h v", v=DVB) for _ in range(2)]
    Sbb = [A.new(BF16, HB * DVB, pat="p (h v) -> p h v", v=DVB) for _ in range(2)]
    x_tok = A.new(F32, NS * D, pat="p (s d) -> p s d", d=D)
    xnT = A.new(BF16, 8 * TN, pat="p (k t) -> p k t", t=TN)
    NSLAB = 2
    wslab = [A.new(BF16, 4096) for _ in range(NSLAB)]
    oaT = A.new(BF16, 16 * TN, pat="p (k t) -> p k t", t=TN)
    obT = A.new(BF16, 16 * TN, pat="p (k t) -> p k t", t=TN)
    mixT = A.new(BF16, 8 * TN, pat="p (k t) -> p k t", t=TN)
    cosv = A.new(F32, TN)
    sinv = A.new(F32, TN)
    ssq = A.new(F32, 2 * NS)
    rstd = A.new(F32, NS)
    gsm = {n: A.new(F32, NCH * 8, parts=64, pat="p (c h) -> p c h", h=8)
           for n in ("e1", "l1", "lnb", "beta", "t", "g")}
    glog = A.new(F32, NCH * 16, parts=64, pat="p (c h) -> p c h", h=16)
    RSZ = max(32 * TN * 2, 32 * TN * 2 + NS * D * 4)
    Roff = A.alloc(RSZ)
    rs = Sub(A, Roff, RSZ)
    qaT = rs.new(BF16, 8 * TN, pat="p (k t) -> p k t", t=TN)
    kaT = rs.new(BF16, 8 * TN, pat="p (k t) -> p k t", t=TN)
    vaT = rs.new(BF16, 16 * TN, pat="p (k t) -> p k t", t=TN)
    qbT = rs.new(BF16, 8 * TN, pat="p (k t) -> p k t", t=TN)
    kbT = rs.new(BF16, 8 * TN, pat="p (k t) -> p k t", t=TN)
    vbT = A.new(BF16, 16 * TN, pat="p (k t) -> p k t", t=TN)
    rs = Sub(A, Roff, RSZ)
    actT = rs.new(BF16, 32 * TN, pat="p (k t) -> p k t", t=TN)
    ysb = rs.new(F32, NS * D, pat="p (s d) -> p s d", d=D)
    NSET = 4
    PSZ = NSET * 4352
    Poff = A.alloc(PSZ)
    p_ = Sub(A, Poff, PSZ)
    xs = p_.new(BF16, D)
    junk = p_.new(BF16, D)
    p_ = Sub(A, Poff, PSZ)
    cpre = [p_.new(F32, TN + 4) for _ in range(NSET)]
    cacc = [p_.new(F32, TN) for _ in range(NSET)]
    csl = [p_.new(F32, TN) for _ in range(NSET)]
    crsl = [p_.new(F32, TN) for _ in range(NSET)]
    p_ = Sub(A, Poff, PSZ)
    rt = [p_.new(F32, TN) for _ in range(12)]
    q_ = A
    Ug = q_.new(F32, 512, parts=64, pat="p (h c) -> p h c", c=64)
    EL = q_.new(F32, 512, parts=64, pat="p (h c) -> p h c", c=64)
    EA = q_.new(F32, 512, parts=64, pat="p (h c) -> p h c", c=64)
    Mb = [q_.new(BF16, 512, parts=64, pat="p (h c) -> p h c", c=64) for _ in range(2)]
    Nb = [q_.new(BF16, 512, parts=64, pat="p (h c) -> p h c", c=64) for _ in range(2)]
    XTb = q_.new(BF16, 512, parts=64, pat="p (h c) -> p h c", c=64)
    gcl = q_.new(F32, 16, parts=64)
    XT = q_.new(F32, 512, parts=64, pat="p (h c) -> p h c", c=64)
    egc = q_.new(F32, 16, parts=64)
    bexp = q_.new(F32, 8, parts=64)
    eld = q_.new(F32, 8)
    _okp = A.alloc(2048)
    kp = A.view(_okp, BF16, 1024, parts=64, pat="p (h d) -> p h d", d=128)
    kdb = A.new(BF16, 1024, parts=64, pat="p (h d) -> p h d", d=256)
    kd = q_.new(BF16, 1024, parts=64, pat="p (h d) -> p h d", d=128)
    _ovb = A.alloc(4096)
    vbk = A.view(_ovb, BF16, 2048, parts=64, pat="p (h v) -> p h v", v=256)
    vtb = A.new(BF16, 2048, parts=64, pat="p (h v) -> p h v", v=512)
    vnew = q_.new(BF16, 2048, parts=64, pat="p (h v) -> p h v", v=256)
    nwT = q_.new(BF16, 512, pat="p (h c) -> p h c", c=64)
    egcr = q_.new(F32, 512, pat="p (h c) -> p h c", c=64)
    qeT = q_.new(BF16, 512, pat="p (h c) -> p h c", c=64)
    _oat = A.alloc(1024)
    attnT = A.view(_oat, BF16, 512, parts=64, pat="p (h c) -> p h c", c=64)
    scT = A.new(BF16, 256, parts=64, pat="p (h c) -> p h c", c=64)
    o_sb = q_.new(F32, 1024, pat="p (k c) -> p k c", c=64)
    o_sq = q_.new(BF16, 1024, pat="p (k c) -> p k c", c=64)
    onesb = q_.new(BF16, 128)
    o_rs = q_.new(F32, 512, pat="p (h c) -> p h c", c=64)
    o_sbB = q_.new(F32, 1024, pat="p (k c) -> p k c", c=64)
    o_sqB = q_.new(BF16, 1024, pat="p (k c) -> p k c", c=64)
    o_rsB = q_.new(F32, 256, pat="p (h c) -> p h c", c=64)
    print("SBUF arena used:", A.top, "of", ARENA, "R", RSZ)
    for _n, _v in (("oaT", oaT), ("obT", obT), ("mixT", mixT), ("xnT", xnT), ("Sa", Sa), ("x_tok", x_tok)):
        _lo = min(a[1] for a in _v.atoms) * ATOM
        DBG[_n] = (_lo, (max(a[1] for a in _v.atoms) + 1) * ATOM - _lo)

    sem_c = T.dsem("const")
    sem_x = T.dsem("x")
    sem_o = T.dsem("o")
    sem_w = [T.dsem(f"w{i}") for i in range(NSLAB)]
    sem_cs = T.dsem("cs")
    sem_cv = T.dsem("cv")

    def pbc(ap_row, n):
        return ap_row.partition_broadcast(n)

    T.dma([(c64v.ap, c64_d), (c128v.ap, c128_d),
           (npost_row.ap, npost_d.to_broadcast([128, D])),
           (mpost_row.ap, mpost_d.to_broadcast([128, D])),
           (dtb_row.ap, dtb_d.to_broadcast([64, 8])),
           (alog_row.ap, alog_d.to_broadcast([64, 8])),
           ], [], [c64v, c128v, npost_row, mpost_row, dtb_row, alog_row], sem_c)
    cstage = A.view(Roff, F32, 128)
    vstage = A.view(Roff + 1024, F32, 128, parts=34)
    T.dma([(cstage.ap, conv_d.rearrange("j (b p) -> (j b) p", p=128)),
           (vstage.ap[0:8, :], npre_d.rearrange("o (k p) -> (o k) p", p=128)),
           (vstage.ap[8:16, :], mpre_d.rearrange("o (k p) -> (o k) p", p=128)),
           (vstage.ap[16:18, :], na_d.rearrange("o (k p) -> (o k) p", p=128)),
           (vstage.ap[18:34, :], nb_d.rearrange("o (k p) -> (o k) p", p=128)),
           ] + [(wsm_f.ap[:, k, :], w_in_d[k * 128:(k + 1) * 128, BETA:BETA + 16]) for k in range(8)],
          [], [cstage, vstage, wsm_f], sem_c)
    b_ = PS()
    pc_ = psv(b_, F32, 128 + 34)
    T.mm(pc_[:, 0:128], cstage, identf, last=False, tr=True)
    T.mm(pc_[:, 128:162], vstage, View(identf.ap[0:34, 0:34], identf.atoms), tr=True)
    T.op("dve", lambda e: e.tensor_copy(cw.ap, pc_.ap[:, 0:128].rearrange("p (j b) -> p j b", j=4)), [pc_], [cw])
    T.op("dve", lambda e: e.tensor_copy(wpre.ap, pc_.ap[:, 128:136]), [pc_], [wpre])
    T.op("dve", lambda e: e.tensor_copy(wmlp.ap, pc_.ap[:, 136:144]), [pc_], [wmlp])
    T.op("dve", lambda e: e.tensor_copy(nav.ap, pc_.ap[:, 144:146]), [pc_], [nav])
    T.op("dve", lambda e: e.tensor_copy(nbv.ap, pc_.ap[:, 146:162]), [pc_], [nbv])
    NSTG = 3
    stg = [A.view(Roff + i * 8192, BF16, 4096) for i in range(NSTG)]
    sem_si = [T.dsem(f"si{i}") for i in range(NSTG)]
    sem_so = [T.dsem(f"so{i}") for i in range(NSTG)]
    wv = {}
    n_ = 0
    for src, dst, name in ((w_in_d, win_b, "in"), (w_bra_d, wbra_b, "bra"),
                           (w_brb_d, wbrb_b, "brb"), (w_out_d, wout_b, "out"),
                           (w_up_d, wup_b, "up"), (w_dn_d, wdn_b, "dn")):
        rows, cols = src.shape
        atoms = []
        for r in range(0, rows, 128):
            for c0 in range(0, cols, 4096):
                cw_ = min(4096, cols - c0)
                i = n_ % NSTG
                n_ += 1
                at = View(dst[r:r + 128, c0:c0 + cw_], (("dram", name, r, c0),))
                atoms.append(at.atoms[0])
                T.dma([(stg[i].ap[:, 0:cw_], src[r:r + 128, c0:c0 + cw_])], [], [stg[i]], sem_si[i], eng="pool")
                T.dma([(at.ap, stg[i].ap[:, 0:cw_])], [stg[i]], [at], sem_so[i])
        wv[name] = View(dst, tuple(atoms))

    T.op("dve", lambda e: e.tensor_copy(identb.ap, identf.ap), [identf], [identb])
    T.op("dve", lambda e: e.tensor_copy(wsm.ap, wsm_f.ap), [wsm_f], [wsm])
    T.op("dve", lambda e: e.tensor_copy(onesb.ap, onesf.ap), [onesf], [onesb])
    T.op("act", lambda e: e.activation(negA.ap, alog_row.ap, AF.Exp), [alog_row], [negA])
    T.op("dve", lambda e: e.tensor_scalar(negA.ap, negA.ap, -1.0, None, ALU.mult), [negA], [negA])
    T.op("pool", lambda e: e.memset(epsc.ap, EPS), [], [epsc])
    T.op("pool", lambda e: e.memset(halo.ap, 0.0), [], [halo])
    T.op("pool", lambda e: e.memset(Sa.ap, 0.0), [], [Sa])
    T.op("pool", lambda e: e.memset(Sab.ap, 0.0), [], [Sab])
    for i in range(2):
        T.op("pool", lambda e, i=i: e.memset(Sb[i].ap, 0.0), [], [Sb[i]])
        T.op("pool", lambda e, i=i: e.memset(Sbb[i].ap, 0.0), [], [Sbb[i]])

    slab_i = [0]

    def load_slab(wkey, dram_ap, buf=None):
        i = slab_i[0] if buf is None else buf
        slab_i[0] = (i + 1) % NSLAB
        free = 1
        for s_ in dram_ap.shape[1:]:
            free *= s_
        buf = wslab[i]
        dst = buf.ap[:, 0:free]
        if len(dram_ap.shape) == 3:
            dst = dst.rearrange("p (k c) -> p k c", c=dram_ap.shape[2])
        T.dma([(dst, dram_ap)], [wv[wkey]], [buf], sem_w[i])
        return View(dst, buf.atoms)

    def rms_to_T(dst, wvec):
        T.op("pool", lambda e: e.memset(ssq.ap, 0.0), [], [ssq])
        for s in range(NS):
            T.op("act", lambda e, s=s: e.activation(junk.ap, x_tok.ap[:, s, :], AF.Square,
                                                   accum_out=ssq.ap[:, s:s + 1]),
                 [x_tok], [junk, ssq])
        rsqrt(rstd, ssq[:, 0:NS], 1.0 / D)
        for s in range(NS):
            T.op("dve", lambda e, s=s: e.tensor_scalar(xs.ap, x_tok.ap[:, s, :], rstd.ap[:, s:s + 1], None,
                                                      ALU.mult), [x_tok, rstd], [xs])
            b = PS()
            pt = psv(b, BF16, 1024, pat="p (k t) -> p k t", t=128)
            for k in range(8):
                T.mm(pt[:, k, :], xs[:, k * 128:(k + 1) * 128], identb, last=(k == 7), tr=True)
            T.op("dve", lambda e, s=s, pt=pt: e.tensor_tensor(
                dst.ap[:, :, s * 128:(s + 1) * 128], pt.ap,
                wvec.ap.unsqueeze(2).to_broadcast([128, 8, 128]), ALU.mult), [pt, wvec], [dst])

    def proj_block(slab, kn, col, rhsT, M=128):
        b = PS()
        pt = psv(b, F32, TN, parts=M)
        for k in range(kn):
            T.mm(pt, slab[:, k, col:col + M], rhsT[:, k, :], start=(k == 0), stop=(k == kn - 1),
                 last=(k == kn - 1))
        return pt

    def win_slab(c0):
        return load_slab("in", win_b[:, c0:c0 + 512].rearrange("(k p) c -> p k c", p=128))

    def cut(tag):
        if upto != tag:
            return False
        T.dma([(out_d[0:TN, :].rearrange("(s p) d -> p s d", p=128), x_tok.ap)], [x_tok], [], sem_o)
        return True

    pools = [(0, 5, [0]), (5, 8, [5])]
    U_, UGT_, NEG1_ = k64("U"), k64("UGT"), k64("NEG1")
    NML_, NMUS_, NMUI_ = k64("NML"), k64("NMUS"), k64("NMUI")
    INTP_ = k64("INTP", HB * 64).re("p (h c) -> p h c", c=64)
    KDEC_ = k64("KDEC", HB)
    QDEC_ = k128("QDEC", HB * 64).re("p (h c) -> p h c", c=64)
    I64 = identf[0:64, 0:64]
    ones64 = onesf[0:64, 0:64]
    ones64x128 = onesf[0:64, 0:128]

    def b3(v, n=8):
        return v.un(1).bc([64, n, 64])

    for ti in range(NT):
        full = ti >= NT_WARM
        t0 = ti * TN
        T.dma([(x_tok.ap, x_d[t0:t0 + TN, :].rearrange("(s p) d -> p s d", p=128))], [], [x_tok], sem_x)
        T.dma([(cosv.ap, cos_d[:, t0:t0 + TN]), (sinv.ap, sin_d[:, t0:t0 + TN])], [], [cosv, sinv], sem_cs)
        if cut("pro"):
            break
        rms_to_T(xnT, wpre)
        if cut("s0"):
            break

        blocks = list(range(32)) if (full or ti == NT_WARM - 1) else list(range(8, 32))
        cur_slab, cur_sl = None, -1
        for n_, blk in enumerate(blocks):
            if blk // 4 != cur_sl:
                cur_sl = blk // 4
                cur_slab = win_slab(cur_sl * 512)
            pt = proj_block(cur_slab, 8, (blk % 4) * 128, xnT)
            j = n_ % NSET
            cp, ac, sl, crs = cpre[j], cacc[j], csl[j], crsl[j]
            sq = cp[:, 0:TN]
            T.op("pool", lambda e, cp=cp, blk=blk: e.tensor_copy(cp.ap[:, 0:3], halo.ap[:, blk, :]), [halo], [cp])
            T.op("act", lambda e, cp=cp, pt=pt: e.activation(cp.ap[:, 3:3 + TN], pt.ap, AF.Copy), [pt], [cp])
            T.op("pool", lambda e, cp=cp, blk=blk: e.tensor_copy(halo.ap[:, blk, :], cp.ap[:, TN:TN + 3]), [cp], [halo])
            T.op("dve", lambda e, cp=cp, ac=ac, blk=blk: e.tensor_scalar(
                ac.ap, cp.ap[:, 0:TN], cw.ap[:, 0, blk:blk + 1], None, ALU.mult), [cp, cw], [ac])
            T.op("dve", lambda e, cp=cp, ac=ac, blk=blk: e.scalar_tensor_tensor(
                ac.ap, cp.ap[:, 1:1 + TN], cw.ap[:, 1, blk:blk + 1], ac.ap, ALU.mult, ALU.add), [cp, cw, ac], [ac])
            for tap in (2, 3):
                T.op("dve", lambda e, cp=cp, ac=ac, blk=blk, tap=tap: e.scalar_tensor_tensor(
                    ac.ap, cp.ap[:, tap:tap + TN], cw.ap[:, tap, blk:blk + 1], ac.ap, ALU.mult, ALU.add),
                    [cp, cw, ac], [ac])
            if blk >= 16:
                dst = vaT[:, blk - 16, :]
                T.op("act", lambda e, ac=ac, dst=dst: e.activation(dst.ap, ac.ap, AF.Silu), [ac], [dst])
            else:
                T.op("act", lambda e, ac=ac, sl=sl: e.activation(sl.ap, ac.ap, AF.Silu), [ac], [sl])
                T.op("act", lambda e, sl=sl, sq=sq: e.activation(sq.ap, sl.ap, AF.Square), [sl], [sq])
                b = PS()
                p2 = psv(b, F32, TN)
                T.mm(p2, onesf, sq)
                rsqrt(crs, p2, 1.0)
                if blk < 8:
                    dst, scl = qaT[:, blk, :], DKA ** -0.5
                else:
                    dst, scl = kaT[:, blk - 8, :], 1.0
                T.op("dve", lambda e, sl=sl, dst=dst, scl=scl: e.scalar_tensor_tensor(
                    dst.ap, sl.ap, scl, crs.ap, ALU.mult, ALU.mult), [sl, crs], [dst])

        if cut("s1a"):
            break
        bg = PS()
        pg = psv(bg, F32, NCH * 16, parts=64, pat="p (c h) -> p c h", h=16)
        for c in range(NCH):
            for k in range(8):
                T.mm(pg[:, c, :], xnT[:, k, c * 64:(c + 1) * 64], wsm[:, k, :], start=(k == 0), stop=(k == 7),
                     last=(k == 7))
        T.op("dve", lambda e: e.tensor_copy(glog.ap, pg.ap), [pg], [glog])
        g_e1, g_l1, g_lnb, g_beta, g_t, g_g = (gsm[n] for n in ("e1", "l1", "lnb", "beta", "t", "g"))
        T.op("act", lambda e: e.activation(g_e1.ap, glog.ap[:, :, 0:8], AF.Exp, scale=-1.0), [glog], [g_e1])
        T.op("act", lambda e: e.activation(g_l1.ap, g_e1.ap, AF.Ln, bias=1.0), [g_e1], [g_l1])
        T.op("dve", lambda e: e.tensor_scalar(g_lnb.ap, g_l1.ap, -1.0, None, ALU.mult), [g_l1], [g_lnb])
        T.op("act", lambda e: e.activation(g_beta.ap, g_l1.ap, AF.Exp, scale=-1.0), [g_l1], [g_beta])
        T.op("dve", lambda e: e.tensor_tensor(g_t.ap, glog.ap[:, :, 8:16],
                                              dtb_row.ap.unsqueeze(1).to_broadcast([64, NCH, 8]), ALU.add),
             [glog, dtb_row], [g_t])
        T.op("act", lambda e: e.activation(g_t.ap, g_t.ap, AF.Exp), [g_t], [g_t])
        T.op("act", lambda e: e.activation(g_t.ap, g_t.ap, AF.Ln, bias=1.0), [g_t], [g_t])
        T.op("dve", lambda e: e.tensor_tensor(g_g.ap, g_t.ap, negA.ap.unsqueeze(1).to_broadcast([64, NCH, 8]),
                                              ALU.mult), [g_t, negA], [g_g])

        for which in ((0, 1) if full else (1,)):
            base = QB if which == 0 else KB
            dstT = qbT if which == 0 else kbT
            for h in range(HB):
                if h % 2 == 0:
                    cur_slab = win_slab(base + (h // 2) * 512)
                p1 = proj_block(cur_slab, 8, (h % 2) * 256, xnT)
                p2 = proj_block(cur_slab, 8, (h % 2) * 256 + 128, xnT)
                t1, t2, ta, tb, tc, td = rt[6 * (h % 2):6 * (h % 2) + 6]
                scl = 1.0 if which == 0 else DKB ** -0.5
                T.op("act", lambda e, p1=p1, scl=scl: e.activation(t1.ap, p1.ap, AF.Identity, scale=scl), [p1], [t1])
                T.op("act", lambda e, p2=p2, scl=scl: e.activation(t2.ap, p2.ap, AF.Identity, scale=scl), [p2], [t2])
                T.op("dve", lambda e: e.tensor_tensor(ta.ap, t1.ap, cosv.ap, ALU.mult), [t1, cosv], [ta])
                T.op("dve", lambda e: e.tensor_tensor(tb.ap, t2.ap, sinv.ap, ALU.mult), [t2, sinv], [tb])
                T.op("dve", lambda e, h=h, dstT=dstT: e.tensor_tensor(dstT.ap[:, 2 * h, :], ta.ap, tb.ap, ALU.subtract),
                     [ta, tb], [dstT])
                T.op("dve", lambda e: e.tensor_tensor(tc.ap, t1.ap, sinv.ap, ALU.mult), [t1, sinv], [tc])
                T.op("dve", lambda e: e.tensor_tensor(td.ap, t2.ap, cosv.ap, ALU.mult), [t2, cosv], [td])
                T.op("dve", lambda e, h=h, dstT=dstT: e.tensor_tensor(dstT.ap[:, 2 * h + 1, :], tc.ap, td.ap, ALU.add),
                     [tc, td], [dstT])
        cur_sl = -1
        for blk in range(16):
            if blk // 4 != cur_sl:
                cur_sl = blk // 4
                cur_slab = win_slab(VB + cur_sl * 512)
            pt = proj_block(cur_slab, 8, (blk % 4) * 128, xnT)
            if blk % 2:
                T.op("act", lambda e, pt=pt, blk=blk: e.activation(vbT.ap[:, blk, :], pt.ap, AF.Copy), [pt], [vbT])
            else:
                T.op("dve", lambda e, pt=pt, blk=blk: e.tensor_copy(vbT.ap[:, blk, :], pt.ap), [pt], [vbT])

        def a_chunk(c):
            cs = slice(c * 64, (c + 1) * 64)
            gc, lnbc, betac = g_g[:, c, :], g_lnb[:, c, :], g_beta[:, c, :]
            yield
            T.op("dve", lambda e, gc=gc: e.tensor_tensor(Ug.ap, b3(U_).ap, gc.ap.unsqueeze(2).to_broadcast([64, 8, 64]),
                                                         ALU.mult), [U_, gc], [Ug])
            b = PS()
            p_s = psv(b, F32, 16, parts=64)
            p_l = psv(b, F32, 512)[:, 256:264]
            T.mm(p_s[:, 0:8], U_, gc, last=False)
            T.mm(p_s[:, 8:16], UGT_, gc, last=False)
            T.mm(p_l, ones64x128, gc)
            bR = PS()
            pR = psv(bR, F32, 512, pat="p (h c) -> p h c", c=64)
            T.mm(View(pR.ap.rearrange("p h c -> p (h c)"), pR.atoms), ones64x128,
                 View(Ug.ap.rearrange("p h c -> p (h c)"), Ug.atoms))
            pR64 = View(pR.ap[0:64], pR.atoms)
            yield
            T.op("act", lambda e, p_s=p_s: e.activation(egc.ap, p_s.ap, AF.Exp), [p_s], [egc])
            yield
            T.op("act", lambda e, p_l=p_l: e.activation(eld.ap, p_l.ap, AF.Exp), [p_l], [eld])
            yield
            T.op("dve", lambda e, p_s=p_s, lnbc=lnbc: e.tensor_tensor(gcl.ap[:, 0:8], p_s.ap[:, 0:8], lnbc.ap, ALU.add),
                 [p_s, lnbc], [gcl])
            yield
            T.op("dve", lambda e, p_s=p_s: e.tensor_copy(gcl.ap[:, 8:16], p_s.ap[:, 0:8]), [p_s], [gcl])
            yield
            T.op("dve", lambda e, betac=betac: e.tensor_tensor(bexp.ap, betac.ap, egc.ap[:, 0:8], ALU.mult),
                 [betac, egc], [bexp])
            yield
            T.op("dve", lambda e, pR64=pR64: e.scalar_tensor_tensor(EL.ap, pR64.ap, -1.0, b3(NML_).ap, ALU.mult, ALU.add),
                 [pR64, NML_], [EL])
            yield
            T.op("dve", lambda e: e.tensor_tensor(EL.ap, EL.ap, gcl.ap[:, 0:8].unsqueeze(2).to_broadcast([64, 8, 64]),
                                                  ALU.add), [EL, gcl], [EL])
            yield
            T.op("act", lambda e: e.activation(EL.ap, EL.ap, AF.Exp), [EL], [EL])
            if full:
                yield
                T.op("dve", lambda e, pR64=pR64: e.tensor_tensor(EA.ap, pR64.ap, b3(NMUI_).ap, ALU.add), [pR64, NMUI_], [EA])
                yield
                T.op("dve", lambda e: e.tensor_tensor(EA.ap, EA.ap, gcl.ap[:, 8:16].unsqueeze(2).to_broadcast([64, 8, 64]),
                                                      ALU.subtract), [EA, gcl], [EA])
                yield
                T.op("act", lambda e: e.activation(EA.ap, EA.ap, AF.Exp), [EA], [EA])
                yield
                T.op("act", lambda e, pR=pR: e.activation(egcr.ap, pR.ap, AF.Exp), [pR], [egcr])
            bG = PS()
            pG = psv(bG, F32, 512, parts=64, pat="p (h c) -> p h c", c=64)
            for h in range(8):
                T.mm(pG[:, h, :], kaT[:, h, cs], kaT[:, h, cs], last=(h == 7))
            bK = PS()
            pK = psv(bK, BF16, 1024, parts=64, pat="p (h d) -> p h d", d=128)
            for h in range(8):
                T.mm(pK[:, h, :], kaT[:, h, cs], identb, last=(h == 7), tr=True)
            yield
            T.op("dve", lambda e, pK=pK: e.tensor_tensor(kp.ap, pK.ap, bexp.ap.unsqueeze(2).to_broadcast([64, 8, 128]),
                                                         ALU.mult), [pK, bexp], [kp])
            yield
            T.op("dve", lambda e, pK=pK: e.tensor_tensor(kd.ap, pK.ap,
                                                         egc.ap[:, 8:16].unsqueeze(2).to_broadcast([64, 8, 128]),
                                                         ALU.mult), [pK, egc], [kd])
            M, N = Mb[0], Nb[0]
            yield
            T.op("dve", lambda e, pG=pG, M=M: e.tensor_tensor(M.ap, pG.ap, EL.ap, ALU.mult), [pG, EL], [M])
            if full:
                yield
                T.op("dve", lambda e, cs=cs: e.tensor_tensor(qeT.ap, qaT.ap[:, :, cs], egcr.ap, ALU.mult),
                     [qaT, egcr], [qeT])
                bT = PS()
                pT = psv(bT, F32, 512, parts=64, pat="p (h c) -> p h c", c=64)
                for h in range(8):
                    T.mm(pT[:, h, :], kaT[:, h, cs], qaT[:, h, cs], last=(h == 7))
                yield
                T.op("dve", lambda e, pT=pT: e.tensor_tensor(attnT.ap, pT.ap, EA.ap, ALU.mult), [pT, EA], [attnT])

            bN0 = PS()
            pN0 = psv(bN0, BF16, 512, parts=64, pat="p (h c) -> p h c", c=64)
            for h in range(8):
                T.mm(pN0[:, h, :], M[:, h, :], View(identb.ap[0:64, 0:64], identb.atoms), last=(h == 7), tr=True)
            yield
            T.op("act", lambda e, pN0=pN0, N=N: e.activation(N.ap, pN0.ap, AF.Copy), [pN0], [N])
            yield
            T.op("dve", lambda e, pN0=pN0: e.tensor_tensor(XT.ap, b3(I64).ap, pN0.ap, ALU.subtract), [I64, pN0], [XT])
            yield
            T.op("dve", lambda e: e.tensor_copy(XTb.ap, XT.ap), [XT], [XTb])
            bM = PS()
            pM = psv(bM, F32, 512, parts=64, pat="p (h c) -> p h c", c=64)
            for h in range(8):
                T.mm(pM[:, h, :], N[:, h, :], M[:, h, :], last=(h == 7))
            bN = PS()
            pN = psv(bN, F32, 512, parts=64, pat="p (h c) -> p h c", c=64)
            for h in range(8):
                T.mm(pN[:, h, :], M[:, h, :], N[:, h, :], last=(h == 7))
            M, N = Mb[1], Nb[1]
            yield
            T.op("act", lambda e, pM=pM, M=M: e.activation(M.ap, pM.ap, AF.Copy), [pM], [M])
            yield
            T.op("dve", lambda e, pN=pN, N=N: e.tensor_copy(N.ap, pN.ap), [pN], [N])
            bV = PS(2)
            pV = psv(bV, BF16, 2048, parts=64, nb=2, pat="p (k d) -> p k d", d=128)
            for k in range(16):
                T.mm(pV[:, k, :], vaT[:, k, cs], identb, last=(k == 15), tr=True)
            yield
            T.op("dve", lambda e, pV=pV, betac=betac: e.tensor_tensor(
                vbk.ap, pV.ap.rearrange("p (h j) d -> p h (j d)", j=2),
                betac.ap.unsqueeze(2).to_broadcast([64, 8, 256]), ALU.mult), [pV, betac], [vbk])
            for k_ in range(1, 6):
                M2, N2 = Mb[(k_ + 1) % 2], Nb[(k_ + 1) % 2]
                if k_ < 5:
                    bM = PS()
                    pM = psv(bM, F32, 512, parts=64, pat="p (h c) -> p h c", c=64)
                    for h in range(8):
                        T.mm(pM[:, h, :], N[:, h, :], M[:, h, :], last=(h == 7))
                if k_ < 4:
                    bN = PS()
                    pN = psv(bN, F32, 512, parts=64, pat="p (h c) -> p h c", c=64)
                    for h in range(8):
                        T.mm(pN[:, h, :], M[:, h, :], N[:, h, :], last=(h == 7))
                bX = PS()
                pX = psv(bX, F32, 512, parts=64, pat="p (h c) -> p h c", c=64)
                for h in range(8):
                    T.mm(pX[:, h, :], M[:, h, :], XTb[:, h, :], last=(h == 7))
                if k_ < 5:
                    yield
                    T.op("act", lambda e, pM=pM, M2=M2: e.activation(M2.ap, pM.ap, AF.Copy), [pM], [M2])
                if k_ < 4:
                    yield
                    T.op("dve", lambda e, pN=pN, N2=N2: e.tensor_copy(N2.ap, pN.ap), [pN], [N2])
                yield
                T.op("dve", lambda e, pX=pX: e.tensor_tensor(XT.ap, XT.ap, pX.ap, ALU.add), [XT, pX], [XT])
                yield
                T.op("dve", lambda e: e.tensor_copy(XTb.ap, XT.ap), [XT], [XTb])
                M, N = M2, N2
            TinvT = XTb
            bW = PS()
            pW = psv(bW, F32, 512, pat="p (h c) -> p h c", c=64)
            for h in range(8):
                T.mm(pW[:, h, :], kp[:, h, :], TinvT[:, h, :], last=(h == 7))
            yield
            T.op("act", lambda e, pW=pW: e.activation(nwT.ap, pW.ap, AF.Identity, scale=-1.0), [pW], [nwT])
            for hh in range(2):
                bN_ = PS(2)
                pVN = psv(bN_, F32, 1024, parts=64, nb=2, pat="p (h v) -> p h v", v=256)
                for h4 in range(4):
                    h = hh * 4 + h4
                    T.mm(pVN[:, h4, :], TinvT[:, h, :], vbk[:, h, :], start=True, stop=False, last=False)
                    T.mm(pVN[:, h4, :], nwT[:, h, :], Sab[:, h, :], start=False, stop=True, last=(h4 == 3))
                yield
                T.op("act", lambda e, pVN=pVN, hh=hh: e.activation(vnew.ap[:, hh * 4:(hh + 1) * 4, :], pVN.ap, AF.Copy),
                     [pVN], [vnew])
            if full:
                bO = PS(2)
                pO = psv(bO, F32, 1024, nb=2, pat="p (k c) -> p k c", c=64)
                for h in range(8):
                    for j in range(2):
                        T.mm(pO[:, h * 2 + j, :], Sab[:, h, j * 128:(j + 1) * 128], qeT[:, h, :],
                             start=True, stop=False, last=False)
                        T.mm(pO[:, h * 2 + j, :], vnew[:, h, j * 128:(j + 1) * 128], attnT[:, h, :],
                             start=False, stop=True, last=(h == 7 and j == 1))
                yield
                T.op("act", lambda e, pO=pO: e.activation(o_sb.ap, pO.ap, AF.Copy), [pO], [o_sb])
                yield
                T.op("act", lambda e: e.activation(o_sq.ap, o_sb.ap, AF.Square), [o_sb], [o_sq])
                bS = PS()
                pS = psv(bS, F32, 512, pat="p (h c) -> p h c", c=64)
                for h in range(8):
                    for j in range(2):
                        T.mm(pS[:, h, :], onesb, o_sq[:, h * 2 + j, :], start=(j == 0), stop=(j == 1),
                             last=(h == 7 and j == 1))
                yield
                rsqrt(o_rs, pS, 1.0 / DVA)
                yield
                T.op("dve", lambda e: e.tensor_tensor(
                    o_sb.ap.rearrange("p (h j) c -> p h j c", j=2), o_sb.ap.rearrange("p (h j) c -> p h j c", j=2),
                    o_rs.ap.unsqueeze(2).to_broadcast([128, 8, 2, 64]), ALU.mult), [o_sb, o_rs], [o_sb])
                yield
                T.op("dve", lambda e, cs=cs: e.tensor_tensor(
                    oaT.ap[:, :, cs].rearrange("p (h j) c -> p h j c", j=2),
                    o_sb.ap.rearrange("p (h j) c -> p h j c", j=2),
                    nav.ap.unsqueeze(1).unsqueeze(3).to_broadcast([128, 8, 2, 64]), ALU.mult), [o_sb, nav], [oaT])
            for hh in range(2):
                bS2 = PS(2)
                pS2 = psv(bS2, F32, 1024, nb=2, pat="p (h v) -> p h v", v=256)
                for h4 in range(4):
                    h = hh * 4 + h4
                    T.mm(pS2[:, h4, :], kd[:, h, :], vnew[:, h, :], last=(h4 == 3))
                for h4 in range(4):
                    h = hh * 4 + h4
                    yield
                    T.op("dve", lambda e, h=h, h4=h4, pS2=pS2: e.scalar_tensor_tensor(
                        Sa.ap[:, h, :], Sa.ap[:, h, :], eld.ap[:, h:h + 1], pS2.ap[:, h4, :], ALU.mult, ALU.add),
                        [Sa, eld, pS2], [Sa])
            yield
            T.op("act", lambda e: e.activation(Sab.ap, Sa.ap, AF.Copy), [Sa], [Sab])

            yield
        def b_chunk(c):
            cs = slice(c * 64, (c + 1) * 64)
            bK = PS()
            pK = psv(bK, BF16, 1024, parts=64, pat="p (h d) -> p h d", d=256)
            for k in range(8):
                T.mm(pK[:, k // 2, (k % 2) * 128:(k % 2 + 1) * 128], kbT[:, k, cs], identb, last=(k == 7), tr=True)
            yield
            T.op("dve", lambda e, pK=pK: e.tensor_tensor(kdb.ap, pK.ap, KDEC_.ap.unsqueeze(2).to_broadcast([64, 4, 256]),
                                                         ALU.mult), [pK, KDEC_], [kdb])
            bV = PS(2)
            pV = psv(bV, BF16, 2048, parts=64, nb=2, pat="p (h v) -> p h v", v=512)
            for k in range(16):
                T.mm(pV[:, k // 4, (k % 4) * 128:(k % 4 + 1) * 128], vbT[:, k, cs], identb, last=(k == 15), tr=True)
            yield
            T.op("act", lambda e, pV=pV: e.activation(vtb.ap, pV.ap, AF.Copy), [pV], [vtb])
            if full:
                bC = PS()
                pC = psv(bC, F32, 256, parts=64, pat="p (h c) -> p h c", c=64)
                for h in range(HB):
                    T.mm(pC[:, h, :], kbT[:, 2 * h, cs], qbT[:, 2 * h, cs], start=True, stop=False, last=False)
                    T.mm(pC[:, h, :], kbT[:, 2 * h + 1, cs], qbT[:, 2 * h + 1, cs], start=False, stop=True,
                         last=(h == HB - 1))
                yield
                T.op("dve", lambda e, pC=pC: e.tensor_tensor(scT.ap, pC.ap, INTP_.ap, ALU.mult), [pC, INTP_], [scT])
                bO = PS(2)
                pO = psv(bO, F32, 1024, nb=2, pat="p (k c) -> p k c", c=64)
                for h in range(HB):
                    for j in range(4):
                        o_ = pO[:, h * 4 + j, :]
                        T.mm(o_, Sbb[0][:, h, j * 128:(j + 1) * 128], qbT[:, 2 * h, cs], start=True, stop=False, last=False)
                        T.mm(o_, Sbb[1][:, h, j * 128:(j + 1) * 128], qbT[:, 2 * h + 1, cs], start=False, stop=False,
                             last=False)
                        T.mm(o_, vtb[:, h, j * 128:(j + 1) * 128], scT[:, h, :], start=False, stop=True,
                             last=(h == HB - 1 and j == 3))
                yield
                T.op("dve", lambda e, pO=pO: e.tensor_tensor(
                    o_sbB.ap.rearrange("p (h j) c -> p h j c", j=4), pO.ap.rearrange("p (h j) c -> p h j c", j=4),
                    QDEC_.ap.unsqueeze(2).to_broadcast([128, 4, 4, 64]), ALU.mult), [pO, QDEC_], [o_sbB])
                yield
                T.op("act", lambda e: e.activation(o_sqB.ap, o_sbB.ap, AF.Square), [o_sbB], [o_sqB])
                bS = PS()
                pS = psv(bS, F32, 256, pat="p (h c) -> p h c", c=64)
                for h in range(HB):
                    for j in range(4):
                        T.mm(pS[:, h, :], onesb, o_sqB[:, h * 4 + j, :], start=(j == 0), stop=(j == 3),
                             last=(h == HB - 1 and j == 3))
                yield
                rsqrt(o_rsB, pS, 1.0 / DVB)
                yield
                T.op("dve", lambda e: e.tensor_tensor(
                    o_sbB.ap.rearrange("p (h j) c -> p h j c", j=4), o_sbB.ap.rearrange("p (h j) c -> p h j c", j=4),
                    o_rsB.ap.unsqueeze(2).to_broadcast([128, 4, 4, 64]), ALU.mult), [o_sbB, o_rsB], [o_sbB])
                yield
                T.op("dve", lambda e, cs=cs: e.tensor_tensor(
                    obT.ap[:, :, cs], o_sbB.ap, nbv.ap.unsqueeze(2).to_broadcast([128, 16, 64]), ALU.mult),
                    [o_sbB, nbv], [obT])
            for h in range(HB):
                bS2 = PS(2)
                for d_ in range(2):
                    pS2 = psv(bS2 + d_, F32, 512)
                    T.mm(pS2, kdb[:, h, d_ * 128:(d_ + 1) * 128], vtb[:, h, :])
                    yield
                    T.op("dve", lambda e, h=h, d_=d_, pS2=pS2: e.scalar_tensor_tensor(
                        Sb[d_].ap[:, h, :], Sb[d_].ap[:, h, :], cdec[h], pS2.ap, ALU.mult, ALU.add),
                        [Sb[d_], pS2], [Sb[d_]])
            yield
            T.op("act", lambda e: e.activation(Sbb[1].ap, Sb[1].ap, AF.Copy), [Sb[1]], [Sbb[1]])
            yield
            T.op("act", lambda e: e.activation(Sbb[0].ap, Sb[0].ap, AF.Copy), [Sb[0]], [Sbb[0]])

            yield
        for c in range(NCH):
            gens = [a_chunk(c), b_chunk(c)]
            alive = [True, True]
            while any(alive):
                for gi, reps in ((0, 2), (1, 1)):
                    for _ in range(reps):
                        if alive[gi]:
                            ps_pool[0] = pools[gi]
                            try:
                                next(gens[gi])
                            except StopIteration:
                                alive[gi] = False
            ps_pool[0] = None
        if full:
            cur_sl = -1
            for blk in range(16):
                if blk // 4 != cur_sl:
                    cur_sl = blk // 4
                    cur_slab = win_slab(ZA + cur_sl * 512)
                pt = proj_block(cur_slab, 8, (blk % 4) * 128, xnT)
                sz = rt[blk % 4]
                T.op("act", lambda e, pt=pt, sz=sz: e.activation(sz.ap, pt.ap, AF.Silu), [pt], [sz])
                T.op("dve", lambda e, blk=blk, sz=sz: e.tensor_tensor(
                    oaT.ap[:, blk, :], oaT.ap[:, blk, :], sz.ap, ALU.mult), [oaT, sz], [oaT])

        if cut("a"):
            break
        if cut("b"):
            break
        if not full:
            continue

        cur_sl = -1
        for blk in range(16):
            if blk // 4 != cur_sl:
                cur_sl = blk // 4
                cur_slab = win_slab(GB + cur_sl * 512)
            pt = proj_block(cur_slab, 8, (blk % 4) * 128, xnT)
            sz = rt[blk % 4]
            T.op("act", lambda e, pt=pt, sz=sz: e.activation(sz.ap, pt.ap, AF.Silu), [pt], [sz])
            T.op("dve", lambda e, blk=blk, sz=sz: e.tensor_tensor(
                obT.ap[:, blk, :], obT.ap[:, blk, :], sz.ap, ALU.mult), [obT, sz], [obT])

        for br in range(2):
            wk, wd, oT, gbase = (("bra", wbra_b, oaT, GTA), ("brb", wbrb_b, obT, GTB))[br]
            for cb in range(8):
                if cb % 2 == 0:
                    sa_ = load_slab(wk, wd[:, cb * 128:cb * 128 + 256].rearrange("(k p) c -> p k c", p=128), buf=0)
                if cb % 4 == 0:
                    sg_ = load_slab("in", win_b[:, gbase + (cb // 4) * 512:gbase + (cb // 4) * 512 + 512]
                                    .rearrange("(k p) c -> p k c", p=128), buf=1)
                pa = proj_block(sa_, 16, (cb % 2) * 128, oT)
                pg_ = proj_block(sg_, 8, (cb % 4) * 128, xnT)
                T.op("act", lambda e, pg_=pg_: e.activation(rt[2 * (cb % 4)].ap, pg_.ap, AF.Sigmoid), [pg_], [rt[2 * (cb % 4)]])
                if br == 0:
                    T.op("dve", lambda e, pa=pa, cb=cb: e.tensor_tensor(mixT.ap[:, cb, :], pa.ap, rt[2 * (cb % 4)].ap, ALU.mult),
                         [pa, rt[2 * (cb % 4)]], [mixT])
                else:
                    T.op("dve", lambda e, pa=pa: e.tensor_tensor(rt[2 * (cb % 4) + 1].ap, pa.ap, rt[2 * (cb % 4)].ap, ALU.mult), [pa, rt[2 * (cb % 4)]], [rt[2 * (cb % 4) + 1]])
                    T.op("dve", lambda e, cb=cb: e.tensor_tensor(mixT.ap[:, cb, :], mixT.ap[:, cb, :], rt[2 * (cb % 4) + 1].ap, ALU.add),
                         [rt[2 * (cb % 4) + 1], mixT], [mixT])

        def tok_proj_norm_res(srcT, kn, wkey, wdram, post_row):
            T.op("pool", lambda e: e.memset(ssq.ap, 0.0), [], [ssq])
            KG = 8
            for half in range(2):
                pts = [psv(PS(), F32, 512) for _ in range(NS)]
                for kg in range(kn // KG):
                    sl_ = load_slab(wkey, wdram[kg * KG * 128:(kg + 1) * KG * 128, half * 512:(half + 1) * 512]
                                    .rearrange("(k p) c -> p k c", p=128))
                    for s in range(NS):
                        for k in range(KG):
                            kk = kg * KG + k
                            T.mm(pts[s], srcT[:, kk, s * 128:(s + 1) * 128], sl_[:, k, :], start=(kk == 0),
                                 stop=(kk == kn - 1), last=(k == KG - 1))
                for s in range(NS):
                    pt = pts[s]
                    T.op("dve", lambda e, pt=pt, half=half, s=s: e.tensor_copy(
                        ysb.ap[:, s, half * 512:(half + 1) * 512], pt.ap), [pt], [ysb])
                    T.op("act", lambda e, half=half, s=s: e.activation(
                        junk.ap[:, 0:512], ysb.ap[:, s, half * 512:(half + 1) * 512], AF.Square,
                        accum_out=ssq.ap[:, 2 * s + half:2 * s + half + 1]), [ysb], [junk, ssq])
            for s in range(NS):
                T.op("dve", lambda e, s=s: e.tensor_tensor(rstd.ap[:, s:s + 1], ssq.ap[:, 2 * s:2 * s + 1],
                                                           ssq.ap[:, 2 * s + 1:2 * s + 2], ALU.add), [ssq], [rstd])
                rsqrt(rstd[:, s:s + 1], rstd[:, s:s + 1], 1.0 / D)
                T.op("dve", lambda e, s=s: e.scalar_tensor_tensor(ysb.ap[:, s, :], ysb.ap[:, s, :], rstd.ap[:, s:s + 1],
                                                                  post_row.ap, ALU.mult, ALU.mult),
                     [ysb, rstd, post_row], [ysb])
                T.op("dve", lambda e, s=s: e.tensor_tensor(x_tok.ap[:, s, :], x_tok.ap[:, s, :], ysb.ap[:, s, :],
                                                            ALU.add), [x_tok, ysb], [x_tok])

        tok_proj_norm_res(mixT, 8, "out", wout_b, npost_row)
        rms_to_T(xnT, wmlp)
        for fb in range(32):
            if fb % 4 == 0:
                su_ = load_slab("up", wup_b[:, fb * 128:fb * 128 + 512].rearrange("(k p) c -> p k c", p=128))
            pt = proj_block(su_, 8, (fb % 4) * 128, xnT)
            r_ = rt[fb % 4]
            T.op("act", lambda e, pt=pt, r_=r_: e.activation(r_.ap, pt.ap, AF.Relu), [pt], [r_])
            T.op("dve", lambda e, fb=fb, r_=r_: e.tensor_tensor(
                actT.ap[:, fb, :], r_.ap, r_.ap, ALU.mult), [r_], [actT])
        tok_proj_norm_res(actT, 32, "dn", wdn_b, mpost_row)
        o0 = (ti - NT_WARM) * TN
        T.dma([(out_d[o0:o0 + TN, :].rearrange("(s p) d -> p s d", p=128), x_tok.ap)], [x_tok], [], sem_o)

    T.wait_all("sp", [sem_o])
    print("instructions:", T.n_inst, "waits:", T.n_wait)
    return nc


_CACHE = {}


def _program(NT, NT_WARM, TN):
    key = (NT, NT_WARM, TN)
    if key not in _CACHE:
        _CACHE[key] = build(NT, NT_WARM, TN)
    return _CACHE[key]


def kernel(x, norm_mix_pre, norm_mix_post, norm_mlp_pre, norm_mlp_post, w_in, conv_a, a_log,
           dt_bias, norm_a, norm_b, w_br_a, w_br_b, w_out, w_up, w_down, _tn=256):
    x = np.asarray(x, np.float32)
    B, S, _ = x.shape
    TN = _tn
    half = S // 2
    NT_WARM = half // TN
    NT = 2 * NT_WARM
    a64, off64, a128, off128, cdec = host_consts()
    f = lambda a: np.ascontiguousarray(np.asarray(a, np.float32))
    shared = {
        "w_in": f(w_in[0]), "w_br_a": f(w_br_a[0]), "w_br_b": f(w_br_b[0]), "w_out": f(w_out[0]),
        "w_up": f(w_up[0]), "w_down": f(w_down[0]), "conv_a": f(conv_a[0]), "a_log": f(a_log),
        "dt_bias": f(dt_bias), "norm_a": f(norm_a), "norm_b": f(norm_b), "norm_mix_pre": f(norm_mix_pre),
        "norm_mix_post": f(norm_mix_post), "norm_mlp_pre": f(norm_mlp_pre), "norm_mlp_post": f(norm_mlp_post),
        "c64": a64, "c128": a128,
    }
    in_maps = []
    for b in range(B):
        for hf in range(2):
            if hf == 0:
                xs = np.concatenate([np.zeros((half, D), np.float32), x[b, :half]], axis=0)
            else:
                xs = x[b]
            rc, rs_ = rope_tables((hf - 1) * half, S)
            m = dict(shared)
            m["x"] = np.ascontiguousarray(xs)
            m["rcos"], m["rsin"] = rc, rs_
            in_maps.append(m)
    nc = _program(NT, NT_WARM, TN)
    res = run_bass_kernel_spmd(nc, in_maps, core_ids=list(range(len(in_maps))))
    out = np.empty((B, S, D), np.float32)
    for b in range(B):
        for hf in range(2):
            out[b, hf * half:(hf + 1) * half] = res.results[b * 2 + hf]["out"]
    return out
```
